# Optimizing a Trainium2 kernel written in Bass

```python
import jax, jax.numpy as jnp
from jax import lax
import numpy as np

D_MODEL = 2048
BATCH = 4
SEQ = 2048
DEPTH = 2

CHUNK = 64
N_LEFT_CHUNKS = 8
BAND = (N_LEFT_CHUNKS + 1) * CHUNK
HEAD_DIM = 64
N_HEADS = D_MODEL // HEAD_DIM
D_FF = 4 * D_MODEL
N_A_LAYERS = DEPTH // 2
N_B_LAYERS = DEPTH - N_A_LAYERS
DECAY_LORA = 96
ICLR_LORA = 96
GATE_LORA = 256
REL_CLIP = 256
RMS_EPS = 1e-6
GN_EPS = 64e-5

kernel_name = "rwkv7_yoco_chunk_band_attn_adaln"


def rms_norm(x, g):
    xf = x.astype(jnp.float32)
    y = xf * lax.rsqrt(jnp.mean(xf * xf, axis=-1, keepdims=True) + RMS_EPS)
    return (y * g.astype(jnp.float32)).astype(x.dtype)


def modulate(h, shift, scale):
    return h * (1 + scale[:, None, :]) + shift[:, None, :]


def token_shift(h):
    return jnp.pad(h, ((0, 0), (1, 0), (0, 0)))[:, :-1]


def rwkv7_time_mix(h, mu, w_r, w_k, w_v, w_o, w0, w1, w2, a0, a1, a2,
                   g1, g2, k_k, k_a, r_k, ln_w, ln_b):
    B, T, D = h.shape
    xx = token_shift(h) - h
    xr, xw, xk, xv, xa, xg = [h + xx * mu[j] for j in range(6)]
    r = xr @ w_r
    k = xk @ w_k
    v = xv @ w_v
    w_log = -jax.nn.softplus(-(w0 + jnp.tanh(xw @ w1) @ w2)) - 0.5
    a = jax.nn.sigmoid(a0 + (xa @ a1) @ a2)
    g = jax.nn.sigmoid(xg @ g1) @ g2

    def heads(t):
        return t.reshape(B, T, N_HEADS, HEAD_DIM).astype(jnp.float32)

    kk = heads(k * k_k)
    kk = kk / jnp.maximum(jnp.sqrt(jnp.sum(kk * kk, axis=-1, keepdims=True)), 1e-12)
    k = k * (1 + (a - 1) * k_a)
    r_h, k_h, v_h, a_h = heads(r), heads(k), heads(v), heads(a)
    decay = jnp.exp(-jnp.exp(heads(w_log)))

    def step(S, inp):
        r_t, w_t, k_t, v_t, kk_t, a_t = inp
        sa = jnp.einsum('bhvk,bhk->bhv', S, -kk_t)
        S = (S * w_t[:, :, None, :] + sa[..., None] * (kk_t * a_t)[:, :, None, :]
             + v_t[..., None] * k_t[:, :, None, :])
        y = jnp.einsum('bhvk,bhk->bhv', S, r_t)
        return S, y

    S0 = jnp.zeros((B, N_HEADS, HEAD_DIM, HEAD_DIM), jnp.float32)
    xs = tuple(jnp.moveaxis(t, 1, 0) for t in (r_h, decay, k_h, v_h, kk, a_h))
    _, y = lax.scan(step, S0, xs)
    y = jnp.moveaxis(y, 0, 1)
    mean = jnp.mean(y, axis=-1, keepdims=True)
    var = jnp.mean(jnp.square(y - mean), axis=-1, keepdims=True)
    y = ((y - mean) * lax.rsqrt(var + GN_EPS) * ln_w.reshape(N_HEADS, HEAD_DIM).astype(jnp.float32)
         + ln_b.reshape(N_HEADS, HEAD_DIM).astype(jnp.float32))
    y = y + jnp.sum(r_h * k_h * r_k.astype(jnp.float32), axis=-1, keepdims=True) * v_h
    y = y.reshape(B, T, D).astype(h.dtype) * g
    return y @ w_o


def chunk_band_attention(h, kp, vp, w_q, w_o, rel_bias):
    B, T, D = h.shape
    nc = T // CHUNK
    q = (h @ w_q).reshape(B, nc, CHUNK, N_HEADS, HEAD_DIM)
    q_chunks = jnp.moveaxis(q, 1, 0)
    pad = N_LEFT_CHUNKS * CHUNK
    i = jnp.arange(CHUNK)[:, None]
    j = jnp.arange(BAND)[None, :]
    rel = i + pad - j
    idx = jnp.clip(rel, -REL_CLIP, REL_CLIP) + REL_CLIP
    bias = rel_bias[:, idx].astype(jnp.float32)
    scale = HEAD_DIM ** -0.5

    def one_chunk(args):
        n, q_c = args
        start = n * CHUNK
        k_b = lax.dynamic_slice_in_dim(kp, start, BAND, axis=1)
        v_b = lax.dynamic_slice_in_dim(vp, start, BAND, axis=1)
        s = jnp.einsum('bqhd,bkhd->bhqk', q_c, k_b).astype(jnp.float32) * scale + bias
        valid = (start + jnp.arange(BAND)) >= pad
        s = jnp.where(valid[None, None, None, :], s, -1e30)
        p = jax.nn.softmax(s, axis=-1).astype(v_b.dtype)
        return jnp.einsum('bhqk,bkhd->bqhd', p, v_b)

    o = lax.map(one_chunk, (jnp.arange(nc), q_chunks))
    o = jnp.moveaxis(o, 0, 1).reshape(B, T, D)
    return o @ w_o


def sq_relu_mlp(h, w_up, w_down):
    return jnp.square(jax.nn.relu(h @ w_up)) @ w_down


def setup_inputs(seed: int = 0) -> dict:
    key = jax.random.key(seed)
    ks = iter(jax.random.split(key, 64))
    D = D_MODEL
    nrm = lambda shape, s: jax.random.normal(next(ks), shape, jnp.float32) * s
    nA, nB, L = N_A_LAYERS, N_B_LAYERS, DEPTH
    return {
        "x": nrm((BATCH, SEQ, D), 1.0),
        "c": nrm((BATCH, D), 1.0),
        "w_ada": nrm((L, D, 6 * D), 0.5 * D ** -0.5),
        "b_ada": nrm((L, 6 * D), 0.01),
        "g_mix": 1.0 + nrm((L, D), 0.02),
        "g_mlp": 1.0 + nrm((L, D), 0.02),
        "w_up": nrm((L, D, D_FF), D ** -0.5),
        "w_down": nrm((L, D_FF, D), D_FF ** -0.5),
        "rwkv_mu": jax.random.uniform(next(ks), (nA, 6, D), jnp.float32),
        "rwkv_w_r": nrm((nA, D, D), D ** -0.5),
        "rwkv_w_k": nrm((nA, D, D), D ** -0.5),
        "rwkv_w_v": nrm((nA, D, D), D ** -0.5),
        "rwkv_w_o": nrm((nA, D, D), D ** -0.5),
        "rwkv_w0": -1.0 + nrm((nA, D), 0.5),
        "rwkv_w1": nrm((nA, D, DECAY_LORA), D ** -0.5),
        "rwkv_w2": nrm((nA, DECAY_LORA, D), 0.5 * DECAY_LORA ** -0.5),
        "rwkv_a0": nrm((nA, D), 0.5),
        "rwkv_a1": nrm((nA, D, ICLR_LORA), D ** -0.5),
        "rwkv_a2": nrm((nA, ICLR_LORA, D), 0.5 * ICLR_LORA ** -0.5),
        "rwkv_g1": nrm((nA, D, GATE_LORA), D ** -0.5),
        "rwkv_g2": nrm((nA, GATE_LORA, D), GATE_LORA ** -0.5),
        "rwkv_k_k": 0.85 + nrm((nA, D), 0.05),
        "rwkv_k_a": 1.0 + nrm((nA, D), 0.05),
        "rwkv_r_k": nrm((nA, N_HEADS, HEAD_DIM), 0.1),
        "rwkv_ln_w": 1.0 + nrm((nA, D), 0.02),
        "rwkv_ln_b": nrm((nA, D), 0.01),
        "attn_w_q": nrm((nB, D, D), D ** -0.5),
        "attn_w_o": nrm((nB, D, D), D ** -0.5),
        "attn_rel_bias": nrm((nB, N_HEADS, 2 * REL_CLIP + 1), 0.5),
        "w_ada_kv": nrm((D, 2 * D), 0.5 * D ** -0.5),
        "b_ada_kv": nrm((2 * D,), 0.01),
        "g_kv": 1.0 + nrm((D,), 0.02),
        "w_k_shared": nrm((D, D), D ** -0.5),
        "w_v_shared": nrm((D, D), D ** -0.5),
        "g_final": 1.0 + nrm((D,), 0.02),
    }


def reference(x, c, w_ada, b_ada, g_mix, g_mlp, w_up, w_down,
              rwkv_mu, rwkv_w_r, rwkv_w_k, rwkv_w_v, rwkv_w_o,
              rwkv_w0, rwkv_w1, rwkv_w2, rwkv_a0, rwkv_a1, rwkv_a2,
              rwkv_g1, rwkv_g2, rwkv_k_k, rwkv_k_a, rwkv_r_k, rwkv_ln_w, rwkv_ln_b,
              attn_w_q, attn_w_o, attn_rel_bias,
              w_ada_kv, b_ada_kv, g_kv, w_k_shared, w_v_shared, g_final):
    B, T, D = x.shape
    pad = N_LEFT_CHUNKS * CHUNK
    kp = vp = None
    for layer in range(DEPTH):
        mod = c @ w_ada[layer] + b_ada[layer]
        sh1, sc1, gt1, sh2, sc2, gt2 = jnp.split(mod, 6, axis=-1)
        if layer == N_A_LAYERS:
            sh_kv, sc_kv = jnp.split(c @ w_ada_kv + b_ada_kv, 2, axis=-1)
            h_kv = modulate(rms_norm(x, g_kv), sh_kv, sc_kv)
            k_s = (h_kv @ w_k_shared).reshape(B, T, N_HEADS, HEAD_DIM)
            v_s = (h_kv @ w_v_shared).reshape(B, T, N_HEADS, HEAD_DIM)
            kp = jnp.pad(k_s, ((0, 0), (pad, 0), (0, 0), (0, 0)))
            vp = jnp.pad(v_s, ((0, 0), (pad, 0), (0, 0), (0, 0)))
        h = modulate(rms_norm(x, g_mix[layer]), sh1, sc1)
        if layer < N_A_LAYERS:
            i = layer
            y = rwkv7_time_mix(h, rwkv_mu[i], rwkv_w_r[i], rwkv_w_k[i], rwkv_w_v[i], rwkv_w_o[i],
                               rwkv_w0[i], rwkv_w1[i], rwkv_w2[i], rwkv_a0[i], rwkv_a1[i], rwkv_a2[i],
                               rwkv_g1[i], rwkv_g2[i], rwkv_k_k[i], rwkv_k_a[i], rwkv_r_k[i],
                               rwkv_ln_w[i], rwkv_ln_b[i])
        else:
            i = layer - N_A_LAYERS
            y = chunk_band_attention(h, kp, vp, attn_w_q[i], attn_w_o[i], attn_rel_bias[i])
        x = x + gt1[:, None, :] * y
        h = modulate(rms_norm(x, g_mlp[layer]), sh2, sc2)
        x = x + gt2[:, None, :] * sq_relu_mlp(h, w_up[layer], w_down[layer])
    return rms_norm(x, g_final)
```

```python
import contextlib
import numpy as np
import concourse.bass as bass
import concourse.mybir as mybir
from concourse.bass_utils import run_bass_kernel_spmd

F32 = mybir.dt.float32
BF16 = mybir.dt.bfloat16
AF = mybir.ActivationFunctionType
ALU = mybir.AluOpType

ENGS = ("pe", "act", "dve", "pool", "sp")
NDMA = 12
SAME_ENG_SYNC = True


class Op:
    __slots__ = ("eng", "fn", "waits", "signal", "sigval", "dma", "dsem", "dval", "idx", "cc")


class Prog:
    def __init__(self, nc):
        self.nc = nc
        self.ops = {e: [] for e in ENGS}
        self.all_ops = []
        self.lastw = {}
        self.readers = {}
        self.ndma = {e: 0 for e in ENGS}
        self.stack = contextlib.ExitStack()
        self.tiles = {}
        self.sb_off = 16576
        self.base = 16576
        self.sb_peak = 0
        self.uid = 0
        self.bar = None
        self.ncc = 0
        self.dmaw = {}

    def sb(self, name, shape, dt):
        nb = 2 if dt == BF16 else 4
        n = 1
        for d in shape[1:]:
            n *= d
        size = (n * nb + 63) // 64 * 64
        off = self.sb_off
        self.sb_off += size
        assert self.sb_off <= 229300, ("SBUF overflow", name, self.sb_off)
        self.sb_peak = max(self.sb_peak, self.sb_off)
        self.uid += 1
        return self.nc.alloc_sbuf_tensor_at("%s_%d" % (name, self.uid), list(shape), dt, offset=off)

    def ps(self, name, shape, dt=F32):
        t = self.stack.enter_context(self.nc.psum_tensor(name, list(shape), dt))
        return t

    def op(self, eng, fn, reads=(), writes=(), dma=False, cc=False):
        import os
        mx = int(os.environ.get("MAXOPS", "0"))
        if mx and len(self.all_ops) >= mx and not (dma and eng == "sp" and not writes):
            return None
        o = Op()
        o.eng = eng
        o.fn = fn
        o.signal = False
        o.sigval = None
        o.dma = dma
        o.cc = cc
        o.waits = []
        o.idx = len(self.all_ops)
        def _nk(k):
            if isinstance(k, str):
                if k.startswith("pq"):
                    return "pq"
                if k.startswith("ptb"):
                    return "ptb"
            return k
        reads = [_nk(k) for k in reads]
        writes = [_nk(k) for k in writes]
        for k in reads:
            if isinstance(k, str) and (k.startswith("pb") or k in ("pq", "ptb")) and k not in writes:
                writes.append(k)
        deps = []
        for k in reads:
            w = self.lastw.get(k)
            if w is not None:
                deps.append((w, "raw"))
            for w2 in self.dmaw.get(k, ()):
                if w2 is not w:
                    deps.append((w2, "raw"))
        for k in writes:
            w = self.lastw.get(k)
            if w is not None and not (dma and w.dma and not self.readers.get(k)):
                deps.append((w, "waw"))
            if not dma:
                for w2 in self.dmaw.get(k, ()):
                    if w2 is not w:
                        deps.append((w2, "waw"))
            for r in self.readers.get(k, {}).values():
                deps.append((r, "war"))
        for (p, kind) in deps:
            if p is o:
                continue
            if p.dma:
                o.waits.append(p)
            elif p.eng == o.eng and not o.dma:
                if p.eng == "pe":
                    continue
                if SAME_ENG_SYNC and kind == "raw":
                    p.signal = True
                    o.waits.append(p)
            else:
                p.signal = True
                o.waits.append(p)
        if self.bar is not None and eng in self.bar:
            for p in self.bar.pop(eng):
                if p.eng != eng or p.dma:
                    if not p.dma:
                        p.signal = True
                    o.waits.append(p)
        for k in reads:
            d = self.readers.setdefault(k, {})
            if dma:
                d[("dma", o.idx)] = o
            else:
                d[eng] = o
        for k in writes:
            if dma:
                if self.readers.get(k) or k not in self.dmaw:
                    self.dmaw[k] = [o]
                else:
                    self.dmaw[k].append(o)
            else:
                self.dmaw[k] = []
            self.lastw[k] = o
            self.readers[k] = {}
        if cc:
            o.dsem = ("c", self.ncc)
            o.dval = 1
            self.ncc += 1
        elif dma:
            n = self.ndma[eng]
            self.ndma[eng] += 1
            o.dsem = n % NDMA
            o.dval = 16 * (n // NDMA + 1)
        self.ops[eng].append(o)
        self.all_ops.append(o)
        return o

    def barrier(self):
        last = []
        for e in ENGS:
            if self.ops[e]:
                last.append(self.ops[e][-1])
            n = self.ndma[e]
            seen = 0
            for o in reversed(self.ops[e]):
                if o.dma:
                    last.append(o)
                    seen += 1
                    if seen >= NDMA:
                        break
        self.bar = {e: list(last) for e in ENGS}

    def emit(self):
        nc = self.nc
        st = self.stack
        esem = {e: st.enter_context(nc.semaphore("s_" + e)) for e in ENGS}
        dsem = {e: [st.enter_context(nc.semaphore("d_%s_%d" % (e, i))) for i in range(NDMA)]
                for e in ENGS if self.ndma[e] > 0}
        csem = [st.enter_context(nc.semaphore("c_%d" % i)) for i in range(self.ncc)]
        for e in ENGS:
            c = 0
            for o in self.ops[e]:
                if o.signal and not o.dma:
                    c += 1
                    o.sigval = c
        st.enter_context(nc.allow_low_precision(reason="bf16 matmul operands by design"))
        block = st.enter_context(nc.Block())

        def run_engine(ename, handle):
            known = {}
            for o in self.ops[ename]:
                need = {}
                for p in o.waits:
                    if p.cc:
                        key = ("c", p.dsem[1])
                        val = 1
                    elif p.dma:
                        key = ("d", p.eng, p.dsem)
                        val = p.dval
                    else:
                        key = ("e", p.eng)
                        val = p.sigval
                    if known.get(key, 0) >= val:
                        continue
                    if need.get(key, 0) < val:
                        need[key] = val
                for key, val in need.items():
                    s = dsem[key[1]][key[2]] if key[0] == "d" else (csem[key[1]] if key[0] == "c" else esem[key[1]])
                    handle.wait_ge(s, val)
                    known[key] = val
                ins = o.fn(handle)
                if o.cc:
                    ins.then_inc(csem[o.dsem[1]])
                elif o.dma:
                    ins.then_inc(dsem[ename][o.dsem], 16)
                elif o.signal:
                    ins.then_inc(esem[ename], 1)
            n = self.ndma[ename]
            if n > 0:
                for i in range(NDMA):
                    cnt = (n - i + NDMA - 1) // NDMA if n > i else 0
                    if cnt > 0:
                        handle.wait_ge(dsem[ename][i], 16 * cnt)

        @block.tensor
        def _(h):
            run_engine("pe", h)

        @block.scalar
        def _(h):
            run_engine("act", h)

        @block.vector
        def _(h):
            run_engine("dve", h)

        @block.gpsimd
        def _(h):
            run_engine("pool", h)

        @block.sync
        def _(h):
            run_engine("sp", h)

    def close(self):
        self.stack.close()


T = 2048
D = 2048
KC = 16
TB = 512
NTB = T // TB
CH = 64
NCH = TB // CH
NPAIR = 8
C1 = 0.6065306597126334


class B:
    def __init__(self, P):
        self.P = P

    def mm(self, out, l, r, start, stop, rd, wr):
        self.P.op("pe", lambda e: e.matmul(out, l, r, start=start, stop=stop), reads=rd, writes=wr)

    def tr(self, out, in_, ident, rd, wr):
        self.P.op("pe", lambda e: e.transpose(out, in_, ident), reads=rd, writes=wr)

    def act(self, out, in_, func, rd, wr, bias=0.0, scale=1.0):
        self.P.op("act", lambda e: e.activation(out=out, in_=in_, func=func, bias=bias, scale=scale),
                  reads=rd, writes=wr)

    def tt(self, eng, out, a, b, op, rd, wr):
        self.P.op(eng, lambda e: e.tensor_tensor(out=out, in0=a, in1=b, op=op), reads=rd, writes=wr)

    def ts(self, eng, out, a, s1, s2, op0, op1, rd, wr):
        if s2 is None:
            self.P.op(eng, lambda e: e.tensor_scalar(out=out, in0=a, scalar1=s1, scalar2=None, op0=op0),
                      reads=rd, writes=wr)
        else:
            self.P.op(eng, lambda e: e.tensor_scalar(out=out, in0=a, scalar1=s1, scalar2=s2, op0=op0, op1=op1),
                      reads=rd, writes=wr)

    def stt(self, out, a, s, b, op0, op1, rd, wr):
        self.P.op("dve", lambda e: e.scalar_tensor_tensor(out=out, in0=a, scalar=s, in1=b, op0=op0, op1=op1),
                  reads=rd, writes=wr)

    def cp(self, eng, out, in_, rd, wr):
        if eng == "act":
            self.act(out, in_, AF.Copy, rd, wr)
        else:
            self.P.op(eng, lambda e: e.tensor_copy(out=out, in_=in_), reads=rd, writes=wr)

    def rcp(self, out, in_, rd, wr):
        self.P.op("dve", lambda e: e.reciprocal(out=out, in_=in_), reads=rd, writes=wr)

    def dma(self, eng, out, in_, rd, wr):
        self.P.op(eng, lambda e: e.dma_start(out=out, in_=in_), reads=rd, writes=wr, dma=True)

    def memset(self, eng, ap, val, wr):
        self.P.op(eng, lambda e: e.memset(ap, val), writes=wr)


def dram_in(nc, name, shape, dt=F32):
    return nc.dram_tensor(name, list(shape), dt, kind="ExternalInput").ap()


def mod_compute(P, b, wada, bcol, ccolb, nchunks, out_tile, psb, stage, stageb, tag):
    wv = wada.rearrange("(c p) n -> p c n", p=128)
    for g in range(nchunks // 2):
        st = stage[g % 2]
        sb_ = stageb[g % 2]
        sk = "%s_st%d" % (tag, g % 2)
        sbk = "%s_sb%d" % (tag, g % 2)
        b.dma("sp", st[:], wv[:, :, g * 256:(g + 1) * 256], [], [sk])
        b.cp("act" if g % 2 == 0 else "dve", sb_[:], st[:], [sk], [sbk])
        for jj in range(2):
            j = g * 2 + jj
            for kc in range(KC):
                b.mm(psb[:, j:j + 1], sb_[:, kc, jj * 128:(jj + 1) * 128], ccolb[:, kc:kc + 1],
                     kc == 0, kc == KC - 1, [sbk, "ccolb"], [tag + "_ps"])
    b.tt("dve", out_tile[:, 0:nchunks], psb[:, 0:nchunks], bcol[:, 0:nchunks], ALU.add,
         [tag + "_ps", tag + "_b"], [tag])


def build_phaseA(nc, debug=False, P=None, sh=None):
    if P is None:
        P = Prog(nc)
    b = B(P)
    xT = dram_in(nc, "xT", [D, T])
    ccol = sh["ccol"] if sh else dram_in(nc, "ccol", [128, KC])
    wada = dram_in(nc, "wada_a", [D, 4096])
    badac = dram_in(nc, "bada_a", [128, 32])
    gmixc = dram_in(nc, "gmix", [128, KC])
    muc = dram_in(nc, "mu", [128, 6, KC])
    w_r = dram_in(nc, "w_r", [D, 1024])
    w_k = dram_in(nc, "w_k", [D, 1024])
    w_v = dram_in(nc, "w_v", [D, 1024])
    w1 = dram_in(nc, "w1", [D, 96])
    a1 = dram_in(nc, "a1", [D, 96])
    g1 = dram_in(nc, "g1", [D, 256])
    w2 = dram_in(nc, "w2", [96, 1024])
    a2 = dram_in(nc, "a2", [96, 1024])
    g2 = dram_in(nc, "g2", [256, 1024])
    vecs = dram_in(nc, "vecs", [128, 7, NPAIR])
    maskG = dram_in(nc, "maskG", [128, 320])
    I2d = sh["I2"] if sh else dram_in(nc, "I2", [128, 64])
    rmaskd = dram_in(nc, "rmask", [128, TB])
    onesd = dram_in(nc, "onesbd", [128, 128])
    yg = None if sh else nc.dram_tensor("yg", [NPAIR * 128, T], BF16, kind="ExternalOutput").ap()

    if sh:
        pb = sh["pbs"][0:6]; ptb = sh["ptb"]; pq = sh["pbs"][6]
    else:
        pb = [P.ps("pb%d" % i, [128, 512]) for i in range(6)]
        ptb = P.ps("ptb", [128, 1024], BF16)
        pq = P.ps("pq", [128, 512])

    hT = P.sb("hT", [128, KC, T + 1], BF16)
    yT = P.sb("yT", [128, 2, T], BF16)
    ccols = P.sb("ccols", [128, KC], F32)
    ccolb = P.sb("ccolb", [128, KC], BF16)
    badas = P.sb("badas", [128, 32], F32)
    gmixs = P.sb("gmixs", [128, KC], F32)
    mus = P.sb("mus", [128, 6, KC], F32)
    omus = P.sb("omus", [128, 6, KC], F32)
    vecss = P.sb("vecss", [128, 7, NPAIR], F32)
    negs = P.sb("negs", [128, 3, NPAIR], F32)
    maskGs = P.sb("maskGs", [128, 320], F32)
    I2b = P.sb("I2b", [128, 64], BF16)
    I2f = P.sb("I2f", [128, 64], F32)
    rmask = P.sb("rmask", [128, TB], F32)
    onesb = P.sb("onesb", [128, 128], BF16)
    modA = P.sb("modA", [128, 32], F32)
    gsc = P.sb("gsc", [128, KC], F32)
    twh = P.sb("twh", [96, T], BF16)
    ahh = P.sb("ahh", [96, T], BF16)
    ghh = P.sb("ghh", [128, 2, T], BF16)
    w2b = P.sb("w2b", [96, 1024], BF16)
    a2b = P.sb("a2b", [96, 1024], BF16)
    g2b = P.sb("g2b", [128, 2, 1024], BF16)
    onesall = P.sb("onesall", [128, 128], BF16)
    b.memset("pool", onesall[:], 1.0, ["ones_all"])
    Hf = P.sb("Hf", [128, 64], F32)
    Hb = P.sb("Hb", [128, 64], BF16)
    HD = P.sb("HD", [128, 64], F32)

    b.dma("sp", ccols[:], ccol, [], ["ccols"])
    b.dma("pool", ccolb[:], ccol, [], ["ccolb"])
    b.dma("sp", badas[:], badac, [], ["modA_b"])
    b.dma("sp", gmixs[:], gmixc, [], ["gmixs"])
    b.dma("sp", mus[:], muc, [], ["mus"])
    b.dma("sp", vecss[:], vecs, [], ["vecss"])
    b.dma("sp", maskGs[:], maskG, [], ["maskGs"])
    b.dma("pool", I2b[:], I2d, [], ["I2b"])
    b.dma("sp", I2f[:], I2d, [], ["I2f"])
    b.dma("sp", rmask[:], rmaskd, [], ["rmask"])
    b.dma("pool", onesb[:], onesd, [], ["onesb"])
    b.dma("pool", w2b[:], w2, [], ["w2b"])
    b.dma("pool", a2b[:], a2, [], ["a2b"])
    b.dma("pool", g2b[:], g2.rearrange("(c p) n -> p c n", p=128), [], ["g2b"])
    b.ts("dve", omus[:], mus[:], -1.0, 1.0, ALU.mult, ALU.add, ["mus"], ["omus"])
    b.ts("dve", negs[:, 0:2, :], vecss[:, 0:2, :], -1.0, None, ALU.mult, None, ["vecss"], ["negs"])
    b.ts("dve", negs[:, 2, :], vecss[:, 3, :], -1.0, 1.0, ALU.mult, ALU.add, ["vecss"], ["negs"])

    mark0 = P.sb_off
    stage = [P.sb("mstage%d" % i, [128, KC, 256], F32) for i in range(2)]
    stageb = [P.sb("mstageb%d" % i, [128, KC, 256], BF16) for i in range(2)]
    mod_compute(P, b, wada, badas, ccolb, 32, modA, pb[0], stage, stageb, "modA")
    b.ts("dve", gsc[:], modA[:, 16:32], 1.0, None, ALU.add, None, ["modA"], ["gsc"])
    b.tt("dve", gsc[:], gsc[:], gmixs[:], ALU.mult, ["gsc", "gmixs"], ["gsc"])

    XB = 256
    xblk = [P.sb("xblk%d" % i, [128, KC, XB], F32) for i in range(2)]
    xsq = P.sb("xsq", [128, KC, XB], BF16)
    rstd = P.sb("rstd", [128, XB], F32)
    b.memset("dve", hT[:, :, 0:1], 0.0, ["hT"])
    xTv = xT.rearrange("(c p) t -> p c t", p=128)
    for i in range(T // XB):
        xb = xblk[i % 2]
        xk = "xblk%d" % (i % 2)
        t0 = i * XB
        b.dma("sp", xb[:], xTv[:, :, t0:t0 + XB], [], [xk] + ["%s_%d" % (xk, kc) for kc in range(KC)])
        b.act(xsq[:], xb[:], AF.Square, [xk], ["xsq"])
        for kc in range(KC):
            b.mm(pb[1][:, 0:XB], onesall[:], xsq[:, kc, :], kc == 0, kc == KC - 1,
                 ["ones_all", "xsq"], ["pb1"])
        b.act(rstd[:], pb[1][:, 0:XB], AF.Ln, ["pb1"], ["rstd"], bias=1e-6, scale=1.0 / D)
        b.act(rstd[:], rstd[:], AF.Exp, ["rstd"], ["rstd"], scale=-0.5)
        for kc in range(KC):
            xkk = "%s_%d" % (xk, kc)
            b.stt(xb[:, kc, :], xb[:, kc, :], gsc[:, kc:kc + 1], rstd[:], ALU.mult, ALU.mult,
                  [xk, "gsc", "rstd"], [xkk])
            b.act(hT[:, kc, 1 + t0:1 + t0 + XB], xb[:, kc, :], AF.Identity, [xkk, "modA"], ["hT"],
                  bias=modA[:, kc:kc + 1])
    P.barrier()
    P.sb_off = mark0
    import os
    if os.environ.get("STOPA") == "1":
        dbg = nc.dram_tensor("dbg", [128, KC, T + 1], BF16, kind="ExternalOutput").ap()
        b.dma("sp", dbg, hT[:], ["hT"], [])
        return P
    phaseA_rest(P, b, locals())
    return P


def phaseA_rest(P, b, L):
    nc = P.nc
    hT = L["hT"]; yT = L["yT"]; mus = L["mus"]; omus = L["omus"]; vecss = L["vecss"]; negs = L["negs"]
    maskGs = L["maskGs"]; I2b = L["I2b"]; I2f = L["I2f"]; rmask = L["rmask"]; onesb = L["onesb"]
    twh = L["twh"]; ahh = L["ahh"]; ghh = L["ghh"]; w2b = L["w2b"]; a2b = L["a2b"]; g2b = L["g2b"]
    Hf = L["Hf"]; Hb = L["Hb"]; HD = L["HD"]; pb = L["pb"]; ptb = L["ptb"]; pq = L["pq"]
    w_r = L["w_r"]; w_k = L["w_k"]; w_v = L["w_v"]; w1 = L["w1"]; a1 = L["a1"]; g1 = L["g1"]; yg = L["yg"]
    mark0 = L["mark0"]
    MU = {"r": 0, "w": 1, "k": 2, "v": 3, "a": 4, "g": 5}

    def fold(src_dram_cols, ncols, j, dstA, dstB, stg, stgk, keyA, keyB):
        b.dma("sp", stg[:, :, 0:ncols], src_dram_cols.rearrange("(c p) n -> p c n", p=128), [], [stgk])
        mub = bass.AP(tensor=mus, offset=j * KC, ap=[[6 * KC, 128], [1, KC], [0, ncols]])
        omub = bass.AP(tensor=omus, offset=j * KC, ap=[[6 * KC, 128], [1, KC], [0, ncols]])
        b.tt("pool", dstA, stg[:, :, 0:ncols], omub, ALU.mult, [stgk, "omus"], [keyA])
        b.tt("dve", dstB, stg[:, :, 0:ncols], mub, ALU.mult, [stgk, "mus"], [keyB])

    stg = P.sb("stgL", [128, KC, 256], F32)
    l1a = P.sb("l1a", [128, KC, 448], BF16)
    l1b = P.sb("l1b", [128, KC, 448], BF16)
    fold(w1, 96, MU["w"], l1a[:, :, 0:96], l1b[:, :, 0:96], stg, "stgL", "l1a", "l1b")
    fold(a1, 96, MU["a"], l1a[:, :, 96:192], l1b[:, :, 96:192], stg, "stgL", "l1a", "l1b")
    fold(g1, 256, MU["g"], l1a[:, :, 192:448], l1b[:, :, 192:448], stg, "stgL", "l1a", "l1b")
    tmpg = P.sb("tmpg", [128, TB], F32)
    for tb in range(NTB):
        t0 = tb * TB
        for (c0, m, which) in ((0, 96, "w"), (96, 96, "a"), (192, 128, "g0"), (320, 128, "g1")):
            bank = pb[2 + (tb * 4 + ("w", "a", "g0", "g1").index(which)) % 2]
            bk = "pbL%d" % ((tb * 4 + ("w", "a", "g0", "g1").index(which)) % 2)
            for kc in range(KC):
                b.mm(bank[0:m, :], l1a[:, kc, c0:c0 + m], hT[:, kc, 1 + t0:1 + t0 + TB], kc == 0, False,
                     ["l1a", "hT"], [bk])
                b.mm(bank[0:m, :], l1b[:, kc, c0:c0 + m], hT[:, kc, t0:t0 + TB], False, kc == KC - 1,
                     ["l1b", "hT"], [bk])
            if which == "w":
                b.act(twh[:, t0:t0 + TB], bank[0:96, :], AF.Tanh, [bk], ["twh"])
            elif which == "a":
                b.cp("dve", ahh[:, t0:t0 + TB], bank[0:96, :], [bk], ["ahh"])
            else:
                gi = 0 if which == "g0" else 1
                b.act(tmpg[:], bank[:, :], AF.Exp, [bk], ["tmpg"], scale=-1.0)
                b.ts("dve", tmpg[:], tmpg[:], 1.0, None, ALU.add, None, ["tmpg"], ["tmpg"])
                b.rcp(ghh[:, gi, t0:t0 + TB], tmpg[:], ["tmpg"], ["ghh"])
    P.barrier()
    P.sb_off = mark0

    import os
    if os.environ.get("STOPA") == "2":
        dbg = nc.dram_tensor("dbg", [96, T], BF16, kind="ExternalOutput").ap()
        b.dma("sp", dbg, twh[:], ["twh"], [])
        return
    stg3 = P.sb("stg3", [128, KC, 128], F32)
    wf = [[P.sb("wf%d_%d" % (q, i), [128, KC, 128], BF16) for i in range(6)] for q in range(1)]
    def f32t(n):
        return P.sb(n, [128, TB], F32)
    def bft(n):
        return P.sb(n, [128, TB], BF16)
    r_f = f32t("r_f"); k_f = f32t("k_f"); v_f = f32t("v_f"); g_f = f32t("g_f")
    sgw = f32t("sgw"); a_f = f32t("a_f"); cs = f32t("cs"); csx = f32t("csx")
    Ep = f32t("Ep"); Em = f32t("Em"); Ex = f32t("Ex"); kk = f32t("kk"); rn = f32t("rn")
    m_f = f32t("m_f"); kp = f32t("kp"); t1 = f32t("t1"); yraw = f32t("yraw"); t2 = f32t("t2")
    v_b = bft("v_b"); kk2 = bft("kk2"); rt = bft("rt"); kt = bft("kt"); bt = bft("bt"); at = bft("at")
    rkb = bft("rkb"); ysq = bft("ysq"); ybf = bft("ybf")
    tm = P.sb("tm", [128, 3, TB], BF16)
    Gs = P.sb("Gs", [128, NCH, 320], BF16)
    Lv = [P.sb("Lv%d" % i, [128, NCH, 2, 64], BF16) for i in range(2)]
    TTn = [P.sb("TTn%d" % i, [128, NCH, 2, 64], BF16) for i in range(2)]
    Ws = P.sb("Ws", [128, 64], BF16)
    Us = P.sb("Us", [128, 64], BF16)
    print("phaseA sbuf peak", P.sb_off)

    hs2 = [slice(0, 64), slice(64, 128)]
    bankctr = [0]

    nrot = 3 if L.get("sh") else 4
    bg = L["sh"].get("bg") if L.get("sh") else None

    def nextbank():
        i = bankctr[0] % nrot
        bankctr[0] += 1
        return pb[i], "pb%d" % i

    for p in range(int(os.environ.get("PA_NP", NPAIR))):
        q = 0
        cols = slice(p * 128, (p + 1) * 128)
        wfk = ["wf%d_%d" % (q, i) for i in range(6)]
        fold(w_r[:, cols], 128, MU["r"], wf[q][0][:], wf[q][1][:], stg3, "stg3", wfk[0], wfk[1])
        fold(w_k[:, cols], 128, MU["k"], wf[q][2][:], wf[q][3][:], stg3, "stg3", wfk[2], wfk[3])
        fold(w_v[:, cols], 128, MU["v"], wf[q][4][:], wf[q][5][:], stg3, "stg3", wfk[4], wfk[5])
        vc = lambda i: vecss[:, i, p:p + 1]
        b.memset("dve", Hf[:], 0.0, ["Hf"])
        b.memset("dve", Hb[:], 0.0, ["Hb"])
        for tb in range(int(os.environ.get("PA_NTB", NTB))):
            t0 = tb * TB
            tsl = slice(t0, t0 + TB)

            def proj(ia, ib):
                bank, bk = nextbank()
                for kc in range(KC):
                    b.mm(bank[:, :], wf[q][ia][:, kc, :], hT[:, kc, 1 + t0:1 + t0 + TB], kc == 0, False,
                         [wfk[ia], "hT"], [bk])
                    b.mm(bank[:, :], wf[q][ib][:, kc, :], hT[:, kc, t0:t0 + TB], False, kc == KC - 1,
                         [wfk[ib], "hT"], [bk])
                return bank, bk
            bank, bk = proj(0, 1)
            b.cp("act", r_f[:], bank[:, :], [bk], ["r_f"])
            bank, bk = proj(2, 3)
            b.cp("dve", k_f[:], bank[:, :], [bk], ["k_f"])
            bank, bk = proj(4, 5)
            b.cp("act", v_f[:], bank[:, :], [bk], ["v_f"])
            b.cp("dve", v_b[:], bank[:, :], [bk], ["v_b"])
            bank, bk = nextbank()
            b.mm(bank[:, :], w2b[:, cols], twh[:, tsl], True, True, ["w2b", "twh"], [bk])
            b.act(sgw[:], bank[:, :], AF.Exp, [bk, "negs"], ["sgw"], bias=negs[:, 0, p:p + 1], scale=-1.0)
            b.ts("dve", sgw[:], sgw[:], 1.0, None, ALU.add, None, ["sgw"], ["sgw"])
            b.rcp(sgw[:], sgw[:], ["sgw"], ["sgw"])
            bank, bk = nextbank()
            b.mm(bank[:, :], a2b[:, cols], ahh[:, tsl], True, True, ["a2b", "ahh"], [bk])
            b.act(a_f[:], bank[:, :], AF.Exp, [bk, "negs"], ["a_f"], bias=negs[:, 1, p:p + 1], scale=-1.0)
            b.ts("dve", a_f[:], a_f[:], 1.0, None, ALU.add, None, ["a_f"], ["a_f"])
            b.rcp(a_f[:], a_f[:], ["a_f"], ["a_f"])
            bank, bk = nextbank()
            b.mm(bank[:, :], g2b[:, 0, cols], ghh[:, 0, tsl], True, False, ["g2b", "ghh"], [bk])
            b.mm(bank[:, :], g2b[:, 1, cols], ghh[:, 1, tsl], False, True, ["g2b", "ghh"], [bk])
            b.cp("act", g_f[:], bank[:, :], [bk], ["g_f"])
            P.op("dve", lambda e: e.tensor_tensor_scan(out=cs[:], data0=rmask[:], data1=sgw[:], initial=0.0,
                                                      op0=ALU.mult, op1=ALU.add),
                 reads=["rmask", "sgw"], writes=["cs"])
            b.tt("pool", csx[:], cs[:], sgw[:], ALU.subtract, ["cs", "sgw"], ["csx"])
            b.act(Ep[:], cs[:], AF.Exp, ["cs"], ["Ep"], scale=-C1)
            b.act(Em[:], cs[:], AF.Exp, ["cs"], ["Em"], scale=C1)
            b.act(Ex[:], csx[:], AF.Exp, ["csx"], ["Ex"], scale=-C1)
            b.act(kk[:], k_f[:], AF.Copy, ["k_f", "vecss"], ["kk"], scale=vc(2))
            b.act(kk2[:], kk[:], AF.Square, ["kk"], ["kk2"])
            b.mm(pb[4][:, :], onesb[:], kk2[:], True, True, ["onesb", "kk2"], ["pb4"])
            b.act(rn[:], pb[4][:, :], AF.Ln, ["pb4"], ["rn"], bias=1e-24)
            b.act(rn[:], rn[:], AF.Exp, ["rn"], ["rn"], scale=-0.5)
            b.tt("dve", kk[:], kk[:], rn[:], ALU.mult, ["kk", "rn"], ["kk"])
            b.ts("dve", m_f[:], a_f[:], vc(3), negs[:, 2, p:p + 1], ALU.mult, ALU.add, ["a_f", "vecss", "negs"], ["m_f"])
            b.tt("dve", kp[:], k_f[:], m_f[:], ALU.mult, ["k_f", "m_f"], ["kp"])
            b.tt("dve", kt[:], kp[:], Em[:], ALU.mult, ["kp", "Em"], ["kt"])
            b.tt("pool", t1[:], kk[:], a_f[:], ALU.mult, ["kk", "a_f"], ["t1"])
            b.tt("dve", bt[:], t1[:], Em[:], ALU.mult, ["t1", "Em"], ["bt"])
            b.stt(at[:], kk[:], -1.0, Ex[:], ALU.mult, ALU.mult, ["kk", "Ex"], ["at"])
            b.tt("pool", rt[:], r_f[:], Ep[:], ALU.mult, ["r_f", "Ep"], ["rt"])
            b.stt(rkb[:], r_f[:], vc(4), kp[:], ALU.mult, ALU.mult, ["r_f", "kp", "vecss"], ["rkb"])
            b.mm(pb[5][:, :], onesb[:], rkb[:], True, True, ["onesb", "rkb"], ["pb5"])
            b.tt("dve", t2[:], pb[5][:, :], v_f[:], ALU.mult, ["pb5", "v_f"], ["t2"])
            for wi, (src, sk) in enumerate(((v_b, "v_b"), (bt, "bt"), (kt, "kt"))):
                half = wi % 2
                for c in range(NCH):
                    for h in range(2):
                        b.tr(ptb[hs2[h], half * 512 + c * 64: half * 512 + (c + 1) * 64],
                             src[hs2[h], c * 64:(c + 1) * 64], I2b[hs2[h], :], [sk, "I2b"], ["ptb%d" % half])
                b.cp("act" if wi != 1 else "dve", tm[:, wi, :], ptb[:, half * 512:(half + 1) * 512],
                     ["ptb%d" % half], ["tm%d" % wi])
            for c in range(NCH):
                csl = slice(c * 64, (c + 1) * 64)
                g4, g4k = (pq, "pq") if c % 2 == 0 else (pb[1], "pb1")
                for h in range(2):
                    H_ = hs2[h]
                    for gi, (l, lk, r, rk) in enumerate(((bt, "bt", at, "at"), (bt, "bt", rt, "rt"),
                                                          (kt, "kt", at, "at"), (kt, "kt", rt, "rt"),
                                                          (at, "at", bt, "bt"))):
                        b.mm(g4[H_, gi * 64:(gi + 1) * 64], l[H_, csl], r[H_, csl], True, True, [lk, rk], [g4k])
                b.tt("dve", Gs[:, c, :], g4[:, 0:320], maskGs[:], ALU.mult, [g4k, "maskGs"], ["Gs%d" % c])
            allG = ["Gs%d" % c for c in range(NCH)]
            I2bc = bass.AP(tensor=I2f, offset=0, ap=[[64, 128], [0, NCH], [1, 64]])
            b.cp("act", Lv[0][:, :, 0, :], Gs[:, :, 0:64], allG, ["Lv0_%d" % c for c in range(NCH)])
            b.cp("act", Lv[0][:, :, 1, :], Gs[:, :, 256:320], allG, ["Lv0_%d" % c for c in range(NCH)])
            b.tt("dve", TTn[0][:, :, 0, :], Gs[:, :, 0:64], I2bc, ALU.add, allG + ["I2f"],
                 ["TT0_%d" % c for c in range(NCH)])
            b.tt("dve", TTn[0][:, :, 1, :], Gs[:, :, 256:320], I2bc, ALU.add, allG + ["I2f"],
                 ["TT0_%d" % c for c in range(NCH)])
            sqb = [(pb[4], "pb4"), (pb[5], "pb5")]
            ttb = [(pq, "pq"), (pb[0], "pb0")]
            for lvl in range(1, 6):
                src = Lv[(lvl - 1) % 2]; dst = Lv[lvl % 2]
                sk = "Lv%d_" % ((lvl - 1) % 2); dk = "Lv%d_" % (lvl % 2)
                tsrc = TTn[(lvl - 1) % 2]; tdst = TTn[lvl % 2]
                tsk = "TT%d_" % ((lvl - 1) % 2); tdk = "TT%d_" % (lvl % 2)
                last = lvl == 5
                if bg and lvl in (1, 3, 5):
                    bg()
                for hf in range(2):
                    bank, bk = sqb[hf]
                    for cc in range(4):
                        c = hf * 4 + cc
                        for h in range(2):
                            H_ = hs2[h]
                            b.mm(bank[H_, cc * 128:cc * 128 + 64], src[H_, c, 1, :], src[H_, c, 0, :], True, True,
                                 [sk + str(c)], [bk])
                            if not last:
                                b.mm(bank[H_, cc * 128 + 64:cc * 128 + 128], src[H_, c, 0, :], src[H_, c, 1, :],
                                     True, True, [sk + str(c)], [bk])
                for hf in range(2):
                    bank, bk = sqb[hf]
                    cs4 = slice(hf * 4, hf * 4 + 4)
                    pv = bank[:, 0:512].rearrange("p (c a b) -> p c a b", a=2, b=64)
                    keys = [dk + str(c) for c in range(hf * 4, hf * 4 + 4)]
                    if last:
                        b.cp("act", dst[:, cs4, 0, :], pv[:, :, 0, :], [bk], keys)
                    else:
                        b.cp("act", dst[:, cs4, :, :], pv, [bk], keys)
                for hf in range(2):
                    bank, bk = ttb[hf]
                    for cc in range(4):
                        c = hf * 4 + cc
                        for h in range(2):
                            H_ = hs2[h]
                            b.mm(bank[H_, cc * 128:cc * 128 + 64], tsrc[H_, c, 1, :], dst[H_, c, 0, :], True, True,
                                 [tsk + str(c), dk + str(c)], [bk])
                            if not last:
                                b.mm(bank[H_, cc * 128 + 64:cc * 128 + 128], tsrc[H_, c, 0, :], dst[H_, c, 1, :],
                                     True, True, [tsk + str(c), dk + str(c)], [bk])
                for hf in range(2):
                    bank, bk = ttb[hf]
                    cs4 = slice(hf * 4, hf * 4 + 4)
                    pv = bank[:, 0:512].rearrange("p (c a b) -> p c a b", a=2, b=64)
                    rk = [bk] + [tsk + str(c) for c in range(hf * 4, hf * 4 + 4)]
                    wkeys = [tdk + str(c) for c in range(hf * 4, hf * 4 + 4)]
                    if last:
                        b.tt("dve", tdst[:, cs4, 0, :], pv[:, :, 0, :], tsrc[:, cs4, 0, :], ALU.add, rk, wkeys)
                    else:
                        b.tt("dve", tdst[:, cs4, :, :], pv, tsrc[:, cs4, :, :], ALU.add, rk, wkeys)
            TTf = TTn[5 % 2]; TTfk = "TT%d_" % (5 % 2)
            for c in range(NCH):
                csl = slice(c * 64, (c + 1) * 64)
                gk = "Gs%d" % c
                dC = Ep[:, c * 64 + 63:c * 64 + 64]
                b.act(HD[:], Hf[:], AF.Copy, ["Hf", "Ep"], ["HD"], scale=dC)
                if bg and c % 3 == 0:
                    bg()
                for h in range(2):
                    H_ = hs2[h]
                    b.mm(pq[H_, 320:384], at[H_, csl], Hb[H_, :], True, False, ["at", "Hb"], ["pqW"])
                    b.mm(pq[H_, 320:384], Gs[H_, c, 128:192], tm[H_, 0, csl], False, True, [gk, "tm0"], ["pqW"])
                b.cp("act", Ws[:], pq[:, 320:384], ["pqW"], ["Ws"])
                for h in range(2):
                    H_ = hs2[h]
                    b.mm(pq[H_, 384:448], TTf[H_, c, 0, :], Ws[H_, :], True, True, [TTfk + str(c), "Ws"], ["pqU"])
                b.cp("act", Us[:], pq[:, 384:448], ["pqU"], ["Us"])
                for h in range(2):
                    H_ = hs2[h]
                    b.mm(pq[H_, 448:512], tm[H_, 1, csl], Us[H_, :], True, False, ["tm1", "Us"], ["pqH"])
                    b.mm(pq[H_, 448:512], tm[H_, 2, csl], tm[H_, 0, csl], False, True, ["tm2", "tm0"], ["pqH"])
                for h in range(2):
                    H_ = hs2[h]
                    yb = pb[c % 2 + 4 - 4 + 0] if False else pb[4 + (c % 2)]
                    ybk = "pb%d" % (4 + c % 2)
                    b.mm(yb[H_, 0:64], Hb[H_, :], rt[H_, csl], True, False, ["Hb", "rt"], [ybk])
                    b.mm(yb[H_, 0:64], Us[H_, :], Gs[H_, c, 64:128], False, False, ["Us", gk], [ybk])
                    b.mm(yb[H_, 0:64], tm[H_, 0, csl], Gs[H_, c, 192:256], False, True, ["tm0", gk], [ybk])
                b.cp("act", yraw[:, csl], pb[4 + (c % 2)][:, 0:64], ["pb%d" % (4 + c % 2)], ["yraw"])
                b.stt(Hb[:], pq[:, 448:512], dC, HD[:], ALU.mult, ALU.add, ["pqH", "Ep", "HD"], ["Hb"])
                b.stt(Hf[:], pq[:, 448:512], dC, HD[:], ALU.mult, ALU.add, ["pqH", "Ep", "HD"], ["Hf"])
            b.cp("act", ybf[:], yraw[:], ["yraw"], ["ybf"])
            b.act(ysq[:], yraw[:], AF.Square, ["yraw"], ["ysq"])
            b.mm(pb[4][:, :], onesb[:], ybf[:], True, True, ["onesb", "ybf"], ["pb4"])
            b.mm(pb[5][:, :], onesb[:], ysq[:], True, True, ["onesb", "ysq"], ["pb5"])
            mean = cs; var = csx; rs = sgw
            b.ts("dve", mean[:], pb[4][:, :], 1.0 / 64, None, ALU.mult, None, ["pb4"], ["cs"])
            b.act(var[:], mean[:], AF.Square, ["cs"], ["csx"])
            b.stt(var[:], pb[5][:, :], 1.0 / 64, var[:], ALU.mult, ALU.subtract, ["pb5", "csx"], ["csx"])
            b.act(rs[:], var[:], AF.Ln, ["csx"], ["sgw"], bias=64e-5)
            b.act(rs[:], rs[:], AF.Exp, ["sgw"], ["sgw"], scale=-0.5)
            b.tt("dve", yraw[:], yraw[:], mean[:], ALU.subtract, ["yraw", "cs"], ["yraw"])
            b.tt("dve", yraw[:], yraw[:], rs[:], ALU.mult, ["yraw", "sgw"], ["yraw"])
            b.ts("dve", yraw[:], yraw[:], vc(5), vc(6), ALU.mult, ALU.add, ["yraw", "vecss"], ["yraw"])
            b.tt("pool", yraw[:], yraw[:], t2[:], ALU.add, ["yraw", "t2"], ["yraw"])
            b.tt("dve", yT[:, p % 2, tsl], yraw[:], g_f[:], ALU.mult, ["yraw", "g_f"], ["yT%d" % (p % 2)])
        if L.get("sh"):
            sh = L["sh"]
            b.dma("sp", sh["ygbufs"][p].ap().rearrange("(c h) w -> c (h w)", h=2), yT[:, p % 2, :],
                  ["yT%d" % (p % 2)], ["ygbuf%d" % p])
            P.op("pool", lambda e, p=p: e.collective_compute(
                "AllGather", mybir.AluOpType.bypass, replica_groups=sh["RG"],
                ins=[sh["ygbufs"][p].ap().opt()], outs=[sh["ygalls"][p].ap().opt()]),
                reads=["ygbuf%d" % p], writes=["ygall%d" % p], dma=True, cc=True)
        else:
            b.dma("sp", yg[p * 128:(p + 1) * 128, :], yT[:, p % 2, :], ["yT%d" % (p % 2)], ["ygbuf%d" % p])


D = 2048
KC = 16
NT = 1024
TB = 512
DFF = 8192


class Lin:
    def __init__(self, P, b, pbanks, wbuf, tag):
        self.P, self.b, self.pb, self.wbuf, self.tag = P, b, pbanks, wbuf, tag
        self.wi = 0
        self.bi = 0

    def run(self, w, kcn, nfc, xfn, xkey, ntb, evac, fc_group=2):
        P, b = self.P, self.b
        wv = w.rearrange("(c p) n -> p c n", p=128)
        nt = kcn // 16
        for g in range(0, nfc, fc_group):
            if nt == 2 and self.wi % 2 == 1:
                self.wi += 1
            tiles = []
            for j in range(nt):
                ti = (self.wi + j) % 4
                tiles.append((self.wbuf[ti], "%s_w%d" % (self.tag, ti)))
            self.wi += nt
            ncol = fc_group * 128
            for j, (wb, wk) in enumerate(tiles):
                b.dma("pool", wb[:, :, 0:ncol], wv[:, j * 16:(j + 1) * 16, g * 128:g * 128 + ncol], [], [wk])
            for fj in range(fc_group):
                fc = g + fj
                for tb in range(ntb):
                    bank = self.pb[self.bi % len(self.pb)]
                    bk = "pb%d" % (self.bi % len(self.pb))
                    self.bi += 1
                    for kc in range(kcn):
                        wb, wk = tiles[kc // 16]
                        b.mm(bank[:, :], wb[:, kc % 16, fj * 128:(fj + 1) * 128], xfn(kc, tb), kc == 0, kc == kcn - 1,
                             [wk, xkey], [bk])
                    evac(fc, tb, bank, bk)


def rmsnorm_mod(P, b, xres, xkey, gsc, shift, out, outkey, pbank, pkey, onesall, sq, rstd, tmp, extra=None):
    for tb in range(NT // TB):
        ts_ = slice(tb * TB, (tb + 1) * TB)
        for kc in range(KC):
            b.act(sq[:, kc % 2, :], xres[:, kc, ts_], AF.Square, [xkey], ["sq%d" % (kc % 2)])
            b.mm(pbank[:, :], onesall[:], sq[:, kc % 2, :], kc == 0, kc == KC - 1, ["ones_all", "sq%d" % (kc % 2)], [pkey])
        b.act(rstd[:], pbank[:, :], AF.Ln, [pkey], ["rstd"], bias=1e-6, scale=1.0 / D)
        b.act(rstd[:], rstd[:], AF.Exp, ["rstd"], ["rstd"], scale=-0.5)
        for kc in range(KC):
            tv = tmp[:, kc % 2, :]
            tk = "tmp%d" % (kc % 2)
            b.stt(tv, xres[:, kc, ts_], gsc[:, kc:kc + 1], rstd[:], ALU.mult, ALU.mult,
                  [xkey, "gsc", "rstd"], [tk])
            if shift is None:
                b.cp("act", out[:, kc, ts_], tv, [tk], [outkey])
            else:
                b.act(out[:, kc, ts_], tv, AF.Identity, [tk, "mod"], [outkey], bias=shift[:, kc:kc + 1])
            if extra is not None:
                gsc2, shift2, out2, out2key, tmp2 = extra
                tv2 = tmp2[:, kc % 2, :]
                tk2 = "tmpx%d" % (kc % 2)
                b.stt(tv2, xres[:, kc, ts_], gsc2[:, kc:kc + 1], rstd[:], ALU.mult, ALU.mult,
                      [xkey, "gsc2", "rstd"], [tk2])
                b.act(out2[:, kc, ts_], tv2, AF.Identity, [tk2, "mod"], [out2key], bias=shift2[:, kc:kc + 1])


def mlp(P, b, lin, hT, hkey, w_up, w_down, xres, xkey, gate, hid):
    for half in range(2):
        def ev_up(fc, tb, bank, bk):
            dst = hid[:, fc, tb * TB:(tb + 1) * TB]
            b.act(dst, bank[:, :], AF.Relu, [bk], ["hid"])
            b.tt("dve", dst, dst, dst, ALU.mult, ["hid"], ["hid"])
        lin.run(w_up[:, half * 4096:(half + 1) * 4096], KC, 32,
                lambda kc, tb: hT[:, kc, tb * TB:(tb + 1) * TB], hkey, NT // TB, ev_up)

        def ev_dn(fc, tb, bank, bk):
            xs = xres[:, fc, tb * TB:(tb + 1) * TB]
            b.stt(xs, bank[:, :], gate[:, fc:fc + 1], xs, ALU.mult, ALU.add, [bk, "mod", xkey], [xkey])
        lin.run(w_down[half * 4096:(half + 1) * 4096, :], 32, KC,
                lambda kc, tb: hid[:, kc, tb * TB:(tb + 1) * TB], "hid", NT // TB, ev_dn, fc_group=2)


def build_phaseB(nc, P=None, sh=None):
    if P is None:
        P = Prog(nc)
    b = B(P)
    xT = dram_in(nc, "xTown", [D, NT])
    ygT = None if sh else dram_in(nc, "ygT", [D, NT], BF16)
    ccol = sh["ccol"] if sh else dram_in(nc, "ccol", [128, KC])
    if not sh:
        wadaB = dram_in(nc, "wada_b", [D, 8192])
        badaB = dram_in(nc, "bada_b", [128, 64])
        wadaK = dram_in(nc, "wada_kv", [D, 4096])
        badaK = dram_in(nc, "bada_kv", [128, 32])
    gvec = dram_in(nc, "gvecB", [128, 3, KC])
    w_o = dram_in(nc, "w_o", [D, D])
    w_up = dram_in(nc, "w_up0", [D, DFF])
    w_down = dram_in(nc, "w_down0", [DFF, D])
    if sh:
        x2T = sh["x2buf"]; hkvT = sh["hkvbuf"]
        pb = sh["pbs"] + [sh["pbs"][0]]
        P.sb_off = P.base
        msel = dram_in(nc, "msel", [128, 2])
    else:
        x2T = nc.dram_tensor("x2T", [D, NT], F32, kind="ExternalOutput").ap()
        hkvT = nc.dram_tensor("hkvT", [D, NT], BF16, kind="ExternalOutput").ap()
        pb = [P.ps("pb%d" % i, [128, 512]) for i in range(8)]
    xres = P.sb("xres", [128, KC, NT], F32)
    hT = P.sb("hT", [128, KC, NT], BF16)
    hid = P.sb("hid", [128, 32, NT], BF16)
    wbuf = [P.sb("wbuf%d" % i, [128, 16, 256], BF16) for i in range(4)]
    ccolb = P.sb("ccolb", [128, KC], BF16)
    badas = P.sb("badas", [128, 64], F32)
    badks = P.sb("badks", [128, 32], F32)
    gvs = P.sb("gvs", [128, 3, KC], F32)
    gsc2 = P.sb("gsc2", [128, KC], F32)
    tmp2 = P.sb("tmp2", [128, 2, TB], F32)
    if sh:
        modB = sh["modall"][:, 0:64]
        modK = sh["modall"][:, 64:96]
    else:
        modB = P.sb("modB", [128, 64], F32)
        modK = P.sb("modK", [128, 32], F32)
    gsc = P.sb("gsc", [128, KC], F32)
    onesall = P.sb("onesall", [128, 128], BF16)
    sq = P.sb("sq", [128, 2, TB], BF16)
    rstd = P.sb("rstd", [128, TB], F32)
    tmp = P.sb("tmp", [128, 2, TB], F32)
    print("phaseB sbuf", P.sb_off)
    b.memset("pool", onesall[:], 1.0, ["ones_all"])
    b.dma("pool", ccolb[:], ccol, [], ["ccolb"])
    if not sh:
        b.dma("sp", badas[:], badaB, [], ["modB_b"])
        b.dma("sp", badks[:], badaK, [], ["modK_b"])
    b.dma("sp", gvs[:], gvec, [], ["gvs"])
    b.dma("sp", xres[:], xT.rearrange("(c p) t -> p c t", p=128), [], ["xres"])
    if sh:
        mss = P.sb("mss", [128, 2], F32)
        b.dma("sp", mss[:], msel, [], ["mss"])
        hT2 = hid[:, 0:KC, :]
        hTv = hT[:].rearrange("p (r q) t -> p r q t", q=8)
        hT2v = hT2.rearrange("p (r q) t -> p r q t", q=8)
        for p in range(8):
            yv = sh["ygalls"][p].ap().rearrange("(r c h) w -> c r h w", c=128, h=2)
            b.dma("sp", hTv[:, :, p, :], yv[:, :, 0, :], ["ygall%d" % p], ["hT"])
            b.dma("sp", hT2v[:, :, p, :], yv[:, :, 1, :], ["ygall%d" % p], ["hid"])
        for kc in range(KC):
            b.act(hT2[:, kc, :], hT2[:, kc, :], AF.Copy, ["hid", "mss"], ["hid"], scale=mss[:, 1:2])
            b.stt(hT[:, kc, :], hT[:, kc, :], mss[:, 0:1], hT2[:, kc, :], ALU.mult, ALU.add,
                  ["hT", "mss", "hid"], ["hT"])
    else:
        b.dma("sp", hT[:], ygT.rearrange("(c p) t -> p c t", p=128), [], ["hT"])
    if not sh:
        _mod(P, b, wadaB, badas, ccolb, 64, modB, pb[0], "pb0", wbuf, "modB")
        _mod(P, b, wadaK, badks, ccolb, 32, modK, pb[1], "pb1", wbuf, "modK")
    lin = Lin(P, b, pb[2:7], wbuf, "lin")
    def ev_o(fc, tb, bank, bk):
        xs = xres[:, fc, tb * TB:(tb + 1) * TB]
        b.stt(xs, bank[:, :], modB[:, fc:fc + 1], xs, ALU.mult, ALU.add, [bk, "modB", "xres"], ["xres"])
    lin.run(w_o, KC, KC, lambda kc, tb: hT[:, kc, tb * TB:(tb + 1) * TB], "hT", NT // TB, ev_o)
    b.ts("dve", gsc[:], modB[:, 32:48], 1.0, None, ALU.add, None, ["modB"], ["gsc"])
    b.tt("dve", gsc[:], gsc[:], gvs[:, 0, :], ALU.mult, ["gsc", "gvs"], ["gsc"])
    rmsnorm_mod(P, b, xres, "xres", gsc, modB[:, 16:32], hT, "hT", pb[0], "pb0", onesall, sq, rstd, tmp)
    mlp(P, b, lin, hT, "hT", w_up, w_down, xres, "xres", modB[:, 48:64], hid)
    b.dma("sp", x2T.rearrange("(c p) t -> p c t", p=128), xres[:], ["xres"], ["x2buf"])
    b.ts("dve", gsc[:], modK[:, 16:32], 1.0, None, ALU.add, None, ["modK"], ["gsc"])
    b.tt("dve", gsc[:], gsc[:], gvs[:, 1, :], ALU.mult, ["gsc", "gvs"], ["gsc"])
    extra = None
    if sh:
        modC = sh["modall"][:, 96:192]
        b.ts("dve", gsc2[:], modC[:, 16:32], 1.0, None, ALU.add, None, ["modall"], ["gsc2"])
        b.tt("dve", gsc2[:], gsc2[:], gvs[:, 2, :], ALU.mult, ["gsc2", "gvs"], ["gsc2"])
        extra = (gsc2, modC[:, 0:16], hid[:, 0:KC, :], "hid", tmp2)
    rmsnorm_mod(P, b, xres, "xres", gsc, modK[:, 0:16], hT, "hT", pb[0], "pb0", onesall, sq, rstd, tmp, extra=extra)
    if sh:
        b.dma("sp", sh["hqbuf"].rearrange("(c p) t -> p c t", p=128), hid[:, 0:KC, :], ["hid"], ["hqbuf"])
    b.dma("sp", hkvT.rearrange("(c p) t -> p c t", p=128), hT[:], ["hT"], ["hkvbuf"])
    if sh:
        for j in range(2):
            b.dma("sp", sh["tailbufs"][j].ap().rearrange("(c p) t -> p c t", p=128), hT[:, j * 8:(j + 1) * 8, 512:1024],
                  ["hT"], ["tailbuf%d" % j])
            P.op("pool", lambda e, j=j: e.collective_compute(
                "AllGather", mybir.AluOpType.bypass, replica_groups=sh["RG"],
                ins=[sh["tailbufs"][j].ap().opt()], outs=[sh["tailalls"][j].ap().opt()]),
                reads=["tailbuf%d" % j], writes=["tailall%d" % j], dma=True, cc=True)
    return P


def _mod(P, b, wada, bcol, ccolb, nchunks, out_tile, psb, pkey, wbuf, tag):
    wv = wada.rearrange("(c p) n -> p c n", p=128)
    for g in range(nchunks // 2):
        st = wbuf[g % 2]
        sk = "lin_w%d" % (g % 2)
        b.dma("pool", st[:, 0:16, :], wv[:, :, g * 256:(g + 1) * 256], [], [sk])
        for jj in range(2):
            j = g * 2 + jj
            for kc in range(KC):
                b.mm(psb[:, j:j + 1], st[:, kc, jj * 128:(jj + 1) * 128], ccolb[:, kc:kc + 1],
                     kc == 0, kc == KC - 1, [sk, "ccolb"], [pkey])
    b.tt("dve", out_tile[:, 0:nchunks], psb[:, 0:nchunks], bcol[:, 0:nchunks], ALU.add,
         [pkey, tag + "_b"], [tag])


D = 2048
KC = 16
NT = 1024
NE = 1536
TB = 512
DFF = 8192
NPAIRC = 16
BAND = 576


def build_phaseC(nc, P=None, sh=None):
    import os
    if P is None:
        P = Prog(nc)
    else:
        P.sb_off = P.base
    b = B(P)
    x2T = sh["x2buf"] if sh else dram_in(nc, "x2T", [D, NT])
    hkvE = None if sh else dram_in(nc, "hkvE", [D, NE], BF16)
    ccol = sh["ccol"] if sh else dram_in(nc, "ccol", [128, KC])
    if not sh:
        wada1 = dram_in(nc, "wada1", [D, 12288])
        bada1 = dram_in(nc, "bada1", [128, 96])
    gvec = dram_in(nc, "gvecC", [128, 3, KC])
    w_q = dram_in(nc, "w_q", [D, D])
    w_k = dram_in(nc, "w_ks", [D, D])
    w_v = dram_in(nc, "w_vs", [D, D])
    w_o = dram_in(nc, "w_ao", [D, D])
    w_up = dram_in(nc, "w_up1", [D, DFF])
    w_down = dram_in(nc, "w_down1", [DFF, D])
    biasd = dram_in(nc, "biasp", [NPAIRC, 128, BAND])
    mrowd = dram_in(nc, "mrow", [128, 1088])
    I2d = sh["I2"] if sh else dram_in(nc, "I2", [128, 64])
    outT = nc.dram_tensor("outT", [D, NT], F32, kind="ExternalOutput").ap()

    if sh:
        pb = sh["pbs"]; ptb = sh["ptb"]
    else:
        pb = [P.ps("pb%d" % i, [128, 512]) for i in range(7)]
        ptb = P.ps("ptb", [128, 1024], BF16)

    ccolb = P.sb("ccolb", [128, KC], BF16)
    badas = P.sb("badas", [128, 96], F32)
    gvs = P.sb("gvs", [128, 3, KC], F32)
    modC = sh["modall"][:, 96:192] if sh else P.sb("modC", [128, 96], F32)
    gsc = P.sb("gsc", [128, KC], F32)
    onesall = P.sb("onesall", [128, 128], BF16)
    I2b = P.sb("I2b", [128, 64], BF16)
    mrow = P.sb("mrow", [128, 1088], F32)
    rstd = P.sb("rstd", [128, TB], F32)
    tmp = P.sb("tmp", [128, 2, TB], F32)
    sq = P.sb("sq", [128, 2, TB], BF16)
    rs = P.sb("rs", [128, 4], F32)
    rinv = P.sb("rinv", [128, 4], F32)
    base = P.sb_off
    R1 = base
    R2 = R1 + 65536
    R4 = R2 + 32768
    R5 = R4 + 32768
    assert R5 + 65536 <= 229300, R5 + 65536

    def at(off, name, shape, dt):
        P.sb_off = off
        t = P.sb(name, shape, dt)
        return t, P.sb_off

    hkv, _ = at(R1, "hkv", [128, KC, NE], BF16)
    xres, _ = at(R1, "xres", [128, KC, NT], F32)
    wbuf = []
    o = R2
    for i in range(4):
        t, o = at(o, "wbuf%d" % i, [128, 16, 256], BF16)
        wbuf.append(t)
    wqkv = []
    o = R2
    for i in range(2):
        row = []
        for j in range(3):
            t, o = at(o, "wqkv%d_%d" % (i, j), [128, KC, 128], BF16)
            row.append(t)
        wqkv.append(row)
    hT, _ = at(R4, "hT", [128, KC, NT], BF16)
    hid, _ = at(R5, "hid", [128, 32, NT], BF16)
    oT, o = at(R5, "oT", [128, KC, NT], BF16)
    QT, o = at(o, "QT", [128, NT], BF16)
    KT, o = at(o, "KT", [128, NE], BF16)
    VT, o = at(o, "VT", [128, NE], BF16)
    Vtm, o = at(o, "Vtm", [128, 24, 64], BF16)
    biasp = []
    for i in range(2):
        t, o = at(o, "biasp%d" % i, [128, BAND], F32)
        biasp.append(t)
    tS = []
    for i in range(3):
        t, o = at(o, "tS%d" % i, [128, BAND], F32)
        tS.append(t)
    Pn = []
    for i in range(3):
        t, o = at(o, "Pn%d" % i, [128, BAND], BF16)
        Pn.append(t)
    PT = []
    for i in range(2):
        t, o = at(o, "PT%d" % i, [128, BAND], BF16)
        PT.append(t)
    assert o <= R5 + 65536, o
    xblk = []
    o = R5
    for i in range(2):
        t, o = at(o, "xblk%d" % i, [128, KC, 256], F32)
        xblk.append(t)
    print("phaseC sbuf end", R5 + 65536)

    b.memset("pool", onesall[:], 1.0, ["ones_all"])
    b.dma("pool", ccolb[:], ccol, [], ["ccolb"])
    if not sh:
        b.dma("sp", badas[:], bada1, [], ["modC_b"])
    b.dma("sp", gvs[:], gvec, [], ["gvs"])
    b.dma("pool", I2b[:], I2d, [], ["I2b"])
    b.dma("sp", mrow[:], mrowd, [], ["mrow"])
    if sh:
        for j in range(2):
            b.dma("sp", hkv[:, j * 8:(j + 1) * 8, 0:512],
                  sh["tailalls"][j].ap()[0:1024, :].rearrange("(c p) t -> p c t", p=128), ["tailall%d" % j], ["hkv"])
        b.dma("sp", hkv[:, :, 512:NE], sh["hkvbuf"].rearrange("(c p) t -> p c t", p=128), ["hkvbuf"], ["hkv"])
    else:
        b.dma("sp", hkv[:], hkvE.rearrange("(c p) t -> p c t", p=128), [], ["hkv"])
    if not sh:
        _mod(P, b, wada1, badas, ccolb, 96, modC, pb[0], "pb0", wbuf, "modC")
    P.barrier()

    x2v = x2T.rearrange("(c p) t -> p c t", p=128)
    if sh:
        b.dma("sp", hT[:], sh["hqbuf"].rearrange("(c p) t -> p c t", p=128), ["hqbuf"], ["hT"])
    b.ts("dve", gsc[:], modC[:, 16:32], 1.0, None, ALU.add, None, ["modC"], ["gsc"])
    b.tt("dve", gsc[:], gsc[:], gvs[:, 0, :], ALU.mult, ["gsc", "gvs"], ["gsc"])
    x2v = x2T.rearrange("(c p) t -> p c t", p=128)
    XB = 256
    for i in range(0 if sh else NT // XB):
        xb = xblk[i % 2]
        xk = "xblk%d" % (i % 2)
        t0 = i * XB
        b.dma("sp", xb[:], x2v[:, :, t0:t0 + XB], ["x2buf"], [xk])
        for kc in range(KC):
            b.act(sq[:, kc % 2, 0:XB], xb[:, kc, :], AF.Square, [xk], ["sq%d" % (kc % 2)])
            b.mm(pb[1][:, 0:XB], onesall[:], sq[:, kc % 2, 0:XB], kc == 0, kc == KC - 1,
                 ["ones_all", "sq%d" % (kc % 2)], ["pb1"])
        b.act(rstd[:, 0:XB], pb[1][:, 0:XB], AF.Ln, ["pb1"], ["rstd"], bias=1e-6, scale=1.0 / D)
        b.act(rstd[:, 0:XB], rstd[:, 0:XB], AF.Exp, ["rstd"], ["rstd"], scale=-0.5)
        for kc in range(KC):
            b.stt(xb[:, kc, :], xb[:, kc, :], gsc[:, kc:kc + 1], rstd[:, 0:XB], ALU.mult, ALU.mult,
                  [xk, "gsc", "rstd"], [xk])
            b.act(hT[:, kc, t0:t0 + XB], xb[:, kc, :], AF.Identity, [xk, "modC"], ["hT"],
                  bias=modC[:, kc:kc + 1])
    P.barrier()

    hs2 = [slice(0, 64), slice(64, 128)]
    bctr = [0]

    def nextbank():
        i = bctr[0] % 4
        bctr[0] += 1
        return pb[i], "pb%d" % i

    npair = int(os.environ.get("PC_NP", NPAIRC))
    for p in range(npair):
        q = p % 2
        cols = slice(p * 128, (p + 1) * 128)
        wk_ = ["wqkv%d_%d" % (q, j) for j in range(3)]
        for j, w in enumerate((w_q, w_k, w_v)):
            b.dma("pool", wqkv[q][j][:], w[:, cols].rearrange("(c p) n -> p c n", p=128), [], [wk_[j]])
        b.dma("sp", biasp[q][:], biasd[p], [], ["biasp%d" % q])
        for tb in range(NT // TB):
            bank, bk = nextbank()
            for kc in range(KC):
                b.mm(bank[:, :], wqkv[q][0][:, kc, :], hT[:, kc, tb * TB:(tb + 1) * TB], kc == 0, kc == KC - 1,
                     [wk_[0], "hT"], [bk])
            b.act(QT[:, tb * TB:(tb + 1) * TB], bank[:, :], AF.Copy, [bk], ["QT"], scale=0.125)
        for tb in range(NE // TB):
            bank, bk = nextbank()
            for kc in range(KC):
                b.mm(bank[:, :], wqkv[q][1][:, kc, :], hkv[:, kc, tb * TB:(tb + 1) * TB], kc == 0, kc == KC - 1,
                     [wk_[1], "hkv"], [bk])
            b.cp("dve", KT[:, tb * TB:(tb + 1) * TB], bank[:, :], [bk], ["KT"])
        for tb in range(NE // TB):
            bank, bk = nextbank()
            for kc in range(KC):
                b.mm(bank[:, :], wqkv[q][2][:, kc, :], hkv[:, kc, tb * TB:(tb + 1) * TB], kc == 0, kc == KC - 1,
                     [wk_[2], "hkv"], [bk])
            b.cp("act", VT[:, tb * TB:(tb + 1) * TB], bank[:, :], [bk], ["VT"])
        for r0 in range(0, 24, 12):
            for bi in range(12):
                blk = r0 + bi
                for h in range(2):
                    b.tr(ptb[hs2[h], bi * 64:(bi + 1) * 64], VT[hs2[h], blk * 64:(blk + 1) * 64], I2b[hs2[h], :],
                         ["VT", "I2b"], ["ptb"])
            b.cp("dve", Vtm[:, r0:r0 + 12, :], ptb[:, 0:768].rearrange("p (a c) -> p a c", c=64), ["ptb"], ["Vtm"])
        NB3 = 3

        def stage_S(n):
            u = n % NB3
            tSk, Pnk = "tS%d" % u, "Pn%d" % u
            qs = slice(n * 64, (n + 1) * 64)
            k0 = n * 64
            for h in range(2):
                H_ = hs2[h]
                b.mm(pb[4][H_, 0:512], QT[H_, qs], KT[H_, k0:k0 + 512], True, True, ["QT", "KT"], ["pb4"])
                b.mm(pb[5][H_, 0:64], QT[H_, qs], KT[H_, k0 + 512:k0 + 576], True, True, ["QT", "KT"], ["pb5"])
            b.tt("dve", tS[u][:, 0:512], pb[4][:, 0:512], biasp[q][:, 0:512], ALU.add,
                 ["pb4", "biasp%d" % q], [tSk])
            b.tt("dve", tS[u][:, 512:576], pb[5][:, 0:64], biasp[q][:, 512:576], ALU.add,
                 ["pb5", "biasp%d" % q], [tSk])
            if n < 8:
                b.tt("pool", tS[u][:], tS[u][:], mrow[:, n * 64:n * 64 + BAND], ALU.add, [tSk, "mrow"], [tSk])
            P.op("act", lambda e, u=u: e.activation(out=tS[u][:], in_=tS[u][:], func=AF.Exp,
                                                    accum_out=rs[:, u:u + 1]),
                 reads=[tSk], writes=[tSk, "rs%d" % u])
            b.rcp(rinv[:, u:u + 1], rs[:, u:u + 1], ["rs%d" % u], ["rinv%d" % u])
            b.ts("dve", Pn[u][:], tS[u][:], rinv[:, u:u + 1], None, ALU.mult, None, [tSk, "rinv%d" % u], [Pnk])

        def stage_TV(n):
            u = n % NB3
            v = n % 2
            Pnk, PTk = "Pn%d" % u, "PT%d" % v
            qs = slice(n * 64, (n + 1) * 64)
            for blk in range(9):
                for h in range(2):
                    b.tr(ptb[hs2[h], blk * 64:(blk + 1) * 64], Pn[u][hs2[h], blk * 64:(blk + 1) * 64],
                         I2b[hs2[h], :], [Pnk, "I2b"], ["ptb"])
            b.cp("act", PT[v][:], ptb[:, 0:BAND], ["ptb"], [PTk])
            for h in range(2):
                H_ = hs2[h]
                for blk in range(9):
                    b.mm(pb[6][H_, 0:64], Vtm[H_, n + blk, :], PT[v][H_, blk * 64:(blk + 1) * 64],
                         blk == 0, blk == 8, ["Vtm", PTk], ["pb6"])
            b.cp("act", oT[:, p, qs], pb[6][:, 0:64], ["pb6"], ["oT"])

        NQ = NT // 64
        stage_S(0)
        stage_S(1)
        for n in range(NQ):
            if n + 2 < NQ:
                stage_S(n + 2)
            stage_TV(n)
    P.barrier()

    if os.environ.get("STOPC") == "1":
        dbg = nc.dram_tensor("dbg", [D, NT], BF16, kind="ExternalOutput").ap()
        b.dma("sp", dbg.rearrange("(c p) t -> p c t", p=128), oT[:], ["oT"], [])
        return P

    b.dma("sp", xres[:], x2v, ["x2buf"], ["xres"])
    lin = Lin(P, b, pb[0:7], wbuf, "lin")

    def ev_o(fc, tb, bank, bk):
        xs = xres[:, fc, tb * TB:(tb + 1) * TB]
        b.stt(xs, bank[:, :], modC[:, 32 + fc:33 + fc], xs, ALU.mult, ALU.add, [bk, "modC", "xres"], ["xres"])
    lin.run(w_o, KC, KC, lambda kc, tb: oT[:, kc, tb * TB:(tb + 1) * TB], "oT", NT // TB, ev_o)
    P.barrier()
    b.ts("dve", gsc[:], modC[:, 64:80], 1.0, None, ALU.add, None, ["modC"], ["gsc"])
    b.tt("dve", gsc[:], gsc[:], gvs[:, 1, :], ALU.mult, ["gsc", "gvs"], ["gsc"])
    rmsnorm_mod(P, b, xres, "xres", gsc, modC[:, 48:64], hT, "hT", pb[0], "pb0", onesall, sq, rstd, tmp)
    mlp(P, b, lin, hT, "hT", w_up, w_down, xres, "xres", modC[:, 80:96], hid)
    b.cp("dve", gsc[:], gvs[:, 2, :], ["gvs"], ["gsc"])
    rmsnorm_mod(P, b, xres, "xres", gsc, None, xres, "xres", pb[0], "pb0", onesall, sq, rstd, tmp)
    b.dma("sp", outT.rearrange("(c p) t -> p c t", p=128), xres[:], ["xres"], [])
    return P


def colv(v, n=None):
    v = np.asarray(v, np.float32)
    return np.ascontiguousarray(v.reshape(-1, 128).T)

def consts():
    r = np.arange(128) % 64
    c = np.arange(64)
    lt = (r[:, None] < c[None, :]).astype(np.float32)
    le = (r[:, None] <= c[None, :]).astype(np.float32)
    gt = (r[:, None] > c[None, :]).astype(np.float32)
    maskG = np.concatenate([lt, le, lt, le, gt], axis=1)
    I2 = (r[:, None] == c[None, :]).astype(np.float32)
    rmask = np.ones((128, 512), np.float32); rmask[:, ::64] = 0.0
    onesbd = np.zeros((128, 128), np.float32); onesbd[:64, :64] = 1; onesbd[64:, 64:] = 1
    return dict(maskG=maskG, I2=I2, rmask=rmask, onesbd=onesbd)

def prepA(inp, b, hh):
    cs = slice(hh * 1024, (hh + 1) * 1024)
    g = lambda k: np.asarray(inp[k], np.float32)
    d = dict(consts())
    d["xT"] = np.ascontiguousarray(g("x")[b].T)
    d["ccol"] = colv(g("c")[b])
    d["wada_a"] = np.ascontiguousarray(g("w_ada")[0][:, 0:4096])
    d["bada_a"] = colv(g("b_ada")[0][0:4096])
    d["gmix"] = colv(g("g_mix")[0])
    mu = g("rwkv_mu")[0]
    d["mu"] = np.ascontiguousarray(np.stack([colv(mu[j]) for j in range(6)], axis=1))
    d["w_r"] = np.ascontiguousarray(g("rwkv_w_r")[0][:, cs])
    d["w_k"] = np.ascontiguousarray(g("rwkv_w_k")[0][:, cs])
    d["w_v"] = np.ascontiguousarray(g("rwkv_w_v")[0][:, cs])
    d["w1"] = g("rwkv_w1")[0]; d["a1"] = g("rwkv_a1")[0]; d["g1"] = g("rwkv_g1")[0]
    d["w2"] = np.ascontiguousarray(g("rwkv_w2")[0][:, cs])
    d["a2"] = np.ascontiguousarray(g("rwkv_a2")[0][:, cs])
    d["g2"] = np.ascontiguousarray(g("rwkv_g2")[0][:, cs])
    vs = [g("rwkv_w0")[0], g("rwkv_a0")[0], g("rwkv_k_k")[0], g("rwkv_k_a")[0],
          g("rwkv_r_k")[0].reshape(-1), g("rwkv_ln_w")[0], g("rwkv_ln_b")[0]]
    d["vecs"] = np.ascontiguousarray(np.stack([colv(v[cs]) for v in vs], axis=1))
    return d

def prepC_common(inp):
    g = lambda k: np.asarray(inp[k], np.float32)
    i = np.arange(64)[:, None]; j = np.arange(576)[None, :]
    idx = np.clip(i + 512 - j, -256, 256) + 256
    bias = g("attn_rel_bias")[0][:, idx]
    d = dict(
        wada1=g("w_ada")[1], bada1=colv(g("b_ada")[1]),
        gvecC=np.ascontiguousarray(np.stack([colv(g("g_mix")[1]), colv(g("g_mlp")[1]), colv(g("g_final"))], axis=1)),
        w_q=g("attn_w_q")[0], w_ks=g("w_k_shared"), w_vs=g("w_v_shared"), w_ao=g("attn_w_o")[0],
        w_up1=g("w_up")[1], w_down1=g("w_down")[1],
        biasp=np.ascontiguousarray(bias.reshape(16, 128, 576)), I2=consts()["I2"])
    return d

def mrow_for(s):
    m = np.zeros((128, 1088), np.float32)
    if s == 0:
        m[:, :512] = -1e30
    return m


RG = [[0, 1], [2, 3], [4, 5], [6, 7]]


def build_fused(nc):
    P = Prog(nc)
    pbs = [P.ps("pb%d" % i, [128, 512]) for i in range(7)]
    ptb = P.ps("ptb", [128, 1024], BF16)
    ygbufs = [nc.dram_tensor("ygbuf%d" % p, [256, 1024], BF16) for p in range(8)]
    ygalls = [nc.dram_tensor("ygall%d" % p, [512, 1024], BF16) for p in range(8)]
    x2buf = nc.dram_tensor("x2buf", [2048, 1024], F32)
    hkvbuf = nc.dram_tensor("hkvbuf", [2048, 1024], BF16)
    hqbuf = nc.dram_tensor("hqbuf", [2048, 1024], BF16)
    tailbufs = [nc.dram_tensor("tailbuf%d" % j, [1024, 512], BF16) for j in range(2)]
    tailalls = [nc.dram_tensor("tailall%d" % j, [2048, 512], BF16) for j in range(2)]
    sh = dict(pbs=pbs, ptb=ptb, ccol=dram_in(nc, "ccol", [128, 16]), I2=dram_in(nc, "I2", [128, 64]),
              ygbufs=ygbufs, ygalls=ygalls, x2buf=x2buf.ap(), hkvbuf=hkvbuf.ap(), hqbuf=hqbuf.ap(),
              tailbufs=tailbufs, tailalls=tailalls, RG=RG)
    b = B(P)
    wsrc = [(dram_in(nc, "wada_b", [2048, 8192]), 64), (dram_in(nc, "wada_kv", [2048, 4096]), 32),
            (dram_in(nc, "wada1", [2048, 12288]), 96)]
    badall = dram_in(nc, "bada_all", [128, 192])
    modall = P.sb("modall", [128, 192], F32)
    badalls = P.sb("badalls", [128, 192], F32)
    ccolbg = P.sb("ccolbg", [128, 16], BF16)
    P.base = P.sb_off
    bgst = [P.sb("bgst%d" % i, [128, 16, 128], BF16) for i in range(2)]
    sh["modall"] = modall
    b.dma("sp", badalls[:], badall, [], ["badalls"])
    b.dma("pool", ccolbg[:], sh["ccol"], [], ["ccolbg"])
    groups = []
    j = 0
    for (w, nch) in wsrc:
        wv = w.rearrange("(c p) n -> p c n", p=128)
        for i in range(nch):
            groups.append((wv[:, :, i * 128:(i + 1) * 128], j))
            j += 1
    pbmod = pbs[3]
    state = [0]

    def bg_dma(gi):
        if gi < len(groups):
            src, col = groups[gi]
            b.dma("pool", bgst[gi % 2][:], src, [], ["bgst%d" % (gi % 2)])

    def bg(n=1):
        for _ in range(n):
            gi = state[0]
            if gi >= len(groups):
                return
            state[0] += 1
            src, col = groups[gi]
            st = bgst[gi % 2]
            sk = "bgst%d" % (gi % 2)
            for kc in range(16):
                b.mm(pbmod[:, col:col + 1], st[:, kc, :], ccolbg[:, kc:kc + 1], kc == 0, kc == 15,
                     [sk, "ccolbg"], ["pb3"])
            bg_dma(gi + 2)
    bg_dma(0)
    bg_dma(1)
    sh["bg"] = bg
    build_phaseA(nc, P=P, sh=sh)
    bg(len(groups))
    b.tt("dve", modall[:], pbmod[:, 0:192], badalls[:], ALU.add, ["pb3", "badalls"], ["modall"])
    P.barrier()
    build_phaseB(nc, P=P, sh=sh)
    P.barrier()
    build_phaseC(nc, P=P, sh=sh)
    P.emit()
    P.close()
    return nc


def kernel(**inp):
    g = lambda k: np.asarray(inp[k], np.float32)
    wada_b = np.ascontiguousarray(g("w_ada")[0][:, 4096:12288])
    bada_all = np.ascontiguousarray(np.concatenate(
        [colv(g("b_ada")[0][4096:12288]), colv(g("b_ada_kv")), colv(g("b_ada")[1])], axis=1))
    gvecB = np.ascontiguousarray(np.stack([colv(g("g_mlp")[0]), colv(g("g_kv")), colv(g("g_mix")[1])], axis=1))
    com = prepC_common(inp)
    maps = []
    for c in range(8):
        b_, r = c // 2, c % 2
        tok = slice(r * 1024, (r + 1) * 1024)
        d = prepA(inp, b_, r)
        d.update(com)
        d.pop("bada1", None)
        msel = np.zeros((128, 2), np.float32)
        msel[:, r] = 1.0
        d.update(dict(
            xTown=np.ascontiguousarray(g("x")[b_, tok].T), wada_b=wada_b,
            wada_kv=g("w_ada_kv"), bada_all=bada_all, gvecB=gvecB,
            w_o=g("rwkv_w_o")[0], w_up0=g("w_up")[0], w_down0=g("w_down")[0],
            msel=msel, mrow=mrow_for(r)))
        maps.append(d)
    nc = bass.Bass("TRN2", target_bir_lowering=False)
    build_fused(nc)
    res = run_bass_kernel_spmd(nc, maps, core_ids=list(range(8))).results
    out = np.zeros((4, 2048, 2048), np.float32)
    for c in range(8):
        b_, r = c // 2, c % 2
        out[b_, r * 1024:(r + 1) * 1024, :] = np.asarray(res[c]["outT"]).T
    return out
```

```python
import contextlib
import numpy as np
import concourse.bass as bass
import concourse.mybir as mybir
from concourse.bass_utils import run_bass_kernel_spmd

F32 = mybir.dt.float32
BF16 = mybir.dt.bfloat16
AF = mybir.ActivationFunctionType
ALU = mybir.AluOpType

ENGS = ("pe", "act", "dve", "pool", "sp")
NDMA = 12
SAME_ENG_SYNC = True


class Op:
    __slots__ = ("eng", "fn", "waits", "signal", "sigval", "dma", "dsem", "dval", "idx", "cc")


class Prog:
    def __init__(self, nc):
        self.nc = nc
        self.ops = {e: [] for e in ENGS}
        self.all_ops = []
        self.lastw = {}
        self.readers = {}
        self.ndma = {e: 0 for e in ENGS}
        self.stack = contextlib.ExitStack()
        self.tiles = {}
        self.sb_off = 16576
        self.base = 16576
        self.sb_peak = 0
        self.uid = 0
        self.bar = None
        self.ncc = 0
        self.dmaw = {}

    def sb(self, name, shape, dt):
        nb = 2 if dt == BF16 else 4
        n = 1
        for d in shape[1:]:
            n *= d
        size = (n * nb + 63) // 64 * 64
        off = self.sb_off
        self.sb_off += size
        assert self.sb_off <= 229300, ("SBUF overflow", name, self.sb_off)
        self.sb_peak = max(self.sb_peak, self.sb_off)
        self.uid += 1
        return self.nc.alloc_sbuf_tensor_at("%s_%d" % (name, self.uid), list(shape), dt, offset=off)

    def ps(self, name, shape, dt=F32):
        t = self.stack.enter_context(self.nc.psum_tensor(name, list(shape), dt))
        return t

    def op(self, eng, fn, reads=(), writes=(), dma=False, cc=False):
        import os
        mx = int(os.environ.get("MAXOPS", "0"))
        if mx and len(self.all_ops) >= mx and not (dma and eng == "sp" and not writes):
            return None
        o = Op()
        o.eng = eng
        o.fn = fn
        o.signal = False
        o.sigval = None
        o.dma = dma
        o.cc = cc
        o.waits = []
        o.idx = len(self.all_ops)
        def _nk(k):
            if isinstance(k, str):
                if k.startswith("pq"):
                    return "pq"
                if k.startswith("ptb"):
                    return "ptb"
            return k
        reads = [_nk(k) for k in reads]
        writes = [_nk(k) for k in writes]
        for k in reads:
            if isinstance(k, str) and (k.startswith("pb") or k in ("pq", "ptb")) and k not in writes:
                writes.append(k)
        deps = []
        for k in reads:
            w = self.lastw.get(k)
            if w is not None:
                deps.append((w, "raw"))
            for w2 in self.dmaw.get(k, ()):
                if w2 is not w:
                    deps.append((w2, "raw"))
        for k in writes:
            w = self.lastw.get(k)
            if w is not None and not (dma and w.dma and not self.readers.get(k)):
                deps.append((w, "waw"))
            if not dma:
                for w2 in self.dmaw.get(k, ()):
                    if w2 is not w:
                        deps.append((w2, "waw"))
            for r in self.readers.get(k, {}).values():
                deps.append((r, "war"))
        for (p, kind) in deps:
            if p is o:
                continue
            if p.dma:
                o.waits.append(p)
            elif p.eng == o.eng and not o.dma:
                if p.eng == "pe":
                    continue
                if SAME_ENG_SYNC and kind == "raw":
                    p.signal = True
                    o.waits.append(p)
            else:
                p.signal = True
                o.waits.append(p)
        if self.bar is not None and eng in self.bar:
            for p in self.bar.pop(eng):
                if p.eng != eng or p.dma:
                    if not p.dma:
                        p.signal = True
                    o.waits.append(p)
        for k in reads:
            d = self.readers.setdefault(k, {})
            if dma:
                d[("dma", o.idx)] = o
            else:
                d[eng] = o
        for k in writes:
            if dma:
                if self.readers.get(k) or k not in self.dmaw:
                    self.dmaw[k] = [o]
                else:
                    self.dmaw[k].append(o)
            else:
                self.dmaw[k] = []
            self.lastw[k] = o
            self.readers[k] = {}
        if cc:
            o.dsem = ("c", self.ncc)
            o.dval = 1
            self.ncc += 1
        elif dma:
            n = self.ndma[eng]
            self.ndma[eng] += 1
            o.dsem = n % NDMA
            o.dval = 16 * (n // NDMA + 1)
        self.ops[eng].append(o)
        self.all_ops.append(o)
        return o

    def barrier(self):
        last = []
        for e in ENGS:
            if self.ops[e]:
                last.append(self.ops[e][-1])
            n = self.ndma[e]
            seen = 0
            for o in reversed(self.ops[e]):
                if o.dma:
                    last.append(o)
                    seen += 1
                    if seen >= NDMA:
                        break
        self.bar = {e: list(last) for e in ENGS}

    def emit(self):
        nc = self.nc
        st = self.stack
        esem = {e: st.enter_context(nc.semaphore("s_" + e)) for e in ENGS}
        dsem = {e: [st.enter_context(nc.semaphore("d_%s_%d" % (e, i))) for i in range(NDMA)]
                for e in ENGS if self.ndma[e] > 0}
        csem = [st.enter_context(nc.semaphore("c_%d" % i)) for i in range(self.ncc)]
        for e in ENGS:
            c = 0
            for o in self.ops[e]:
                if o.signal and not o.dma:
                    c += 1
                    o.sigval = c
        st.enter_context(nc.allow_low_precision(reason="bf16 matmul operands by design"))
        block = st.enter_context(nc.Block())

        def run_engine(ename, handle):
            known = {}
            for o in self.ops[ename]:
                need = {}
                for p in o.waits:
                    if p.cc:
                        key = ("c", p.dsem[1])
                        val = 1
                    elif p.dma:
                        key = ("d", p.eng, p.dsem)
                        val = p.dval
                    else:
                        key = ("e", p.eng)
                        val = p.sigval
                    if known.get(key, 0) >= val:
                        continue
                    if need.get(key, 0) < val:
                        need[key] = val
                for key, val in need.items():
                    s = dsem[key[1]][key[2]] if key[0] == "d" else (csem[key[1]] if key[0] == "c" else esem[key[1]])
                    handle.wait_ge(s, val)
                    known[key] = val
                ins = o.fn(handle)
                if o.cc:
                    ins.then_inc(csem[o.dsem[1]])
                elif o.dma:
                    ins.then_inc(dsem[ename][o.dsem], 16)
                elif o.signal:
                    ins.then_inc(esem[ename], 1)
            n = self.ndma[ename]
            if n > 0:
                for i in range(NDMA):
                    cnt = (n - i + NDMA - 1) // NDMA if n > i else 0
                    if cnt > 0:
                        handle.wait_ge(dsem[ename][i], 16 * cnt)

        @block.tensor
        def _(h):
            run_engine("pe", h)

        @block.scalar
        def _(h):
            run_engine("act", h)

        @block.vector
        def _(h):
            run_engine("dve", h)

        @block.gpsimd
        def _(h):
            run_engine("pool", h)

        @block.sync
        def _(h):
            run_engine("sp", h)

    def close(self):
        self.stack.close()


T = 2048
D = 2048
KC = 16
TB = 512
NTB = T // TB
CH = 64
NCH = TB // CH
NPAIR = 8
C1 = 0.6065306597126334


class B:
    def __init__(self, P):
        self.P = P

    def mm(self, out, l, r, start, stop, rd, wr):
        self.P.op("pe", lambda e: e.matmul(out, l, r, start=start, stop=stop), reads=rd, writes=wr)

    def tr(self, out, in_, ident, rd, wr):
        self.P.op("pe", lambda e: e.transpose(out, in_, ident), reads=rd, writes=wr)

    def act(self, out, in_, func, rd, wr, bias=0.0, scale=1.0):
        self.P.op("act", lambda e: e.activation(out=out, in_=in_, func=func, bias=bias, scale=scale),
                  reads=rd, writes=wr)

    def tt(self, eng, out, a, b, op, rd, wr):
        self.P.op(eng, lambda e: e.tensor_tensor(out=out, in0=a, in1=b, op=op), reads=rd, writes=wr)

    def ts(self, eng, out, a, s1, s2, op0, op1, rd, wr):
        if s2 is None:
            self.P.op(eng, lambda e: e.tensor_scalar(out=out, in0=a, scalar1=s1, scalar2=None, op0=op0),
                      reads=rd, writes=wr)
        else:
            self.P.op(eng, lambda e: e.tensor_scalar(out=out, in0=a, scalar1=s1, scalar2=s2, op0=op0, op1=op1),
                      reads=rd, writes=wr)

    def stt(self, out, a, s, b, op0, op1, rd, wr):
        self.P.op("dve", lambda e: e.scalar_tensor_tensor(out=out, in0=a, scalar=s, in1=b, op0=op0, op1=op1),
                  reads=rd, writes=wr)

    def cp(self, eng, out, in_, rd, wr):
        if eng == "act":
            self.act(out, in_, AF.Copy, rd, wr)
        else:
            self.P.op(eng, lambda e: e.tensor_copy(out=out, in_=in_), reads=rd, writes=wr)

    def rcp(self, out, in_, rd, wr):
        self.P.op("dve", lambda e: e.reciprocal(out=out, in_=in_), reads=rd, writes=wr)

    def dma(self, eng, out, in_, rd, wr):
        self.P.op(eng, lambda e: e.dma_start(out=out, in_=in_), reads=rd, writes=wr, dma=True)

    def memset(self, eng, ap, val, wr):
        self.P.op(eng, lambda e: e.memset(ap, val), writes=wr)


def dram_in(nc, name, shape, dt=F32):
    return nc.dram_tensor(name, list(shape), dt, kind="ExternalInput").ap()


def mod_compute(P, b, wada, bcol, ccolb, nchunks, out_tile, psb, stage, stageb, tag):
    wv = wada.rearrange("(c p) n -> p c n", p=128)
    for g in range(nchunks // 2):
        st = stage[g % 2]
        sb_ = stageb[g % 2]
        sk = "%s_st%d" % (tag, g % 2)
        sbk = "%s_sb%d" % (tag, g % 2)
        b.dma("sp", st[:], wv[:, :, g * 256:(g + 1) * 256], [], [sk])
        b.cp("act" if g % 2 == 0 else "dve", sb_[:], st[:], [sk], [sbk])
        for jj in range(2):
            j = g * 2 + jj
            for kc in range(KC):
                b.mm(psb[:, j:j + 1], sb_[:, kc, jj * 128:(jj + 1) * 128], ccolb[:, kc:kc + 1],
                     kc == 0, kc == KC - 1, [sbk, "ccolb"], [tag + "_ps"])
    b.tt("dve", out_tile[:, 0:nchunks], psb[:, 0:nchunks], bcol[:, 0:nchunks], ALU.add,
         [tag + "_ps", tag + "_b"], [tag])


def build_phaseA(nc, debug=False, P=None, sh=None):
    if P is None:
        P = Prog(nc)
    b = B(P)
    xT = dram_in(nc, "xT", [D, T])
    ccol = sh["ccol"] if sh else dram_in(nc, "ccol", [128, KC])
    wada = dram_in(nc, "wada_a", [D, 4096])
    badac = dram_in(nc, "bada_a", [128, 32])
    gmixc = dram_in(nc, "gmix", [128, KC])
    muc = dram_in(nc, "mu", [128, 6, KC])
    w_r = dram_in(nc, "w_r", [D, 1024])
    w_k = dram_in(nc, "w_k", [D, 1024])
    w_v = dram_in(nc, "w_v", [D, 1024])
    w1 = dram_in(nc, "w1", [D, 96])
    a1 = dram_in(nc, "a1", [D, 96])
    g1 = dram_in(nc, "g1", [D, 256])
    w2 = dram_in(nc, "w2", [96, 1024])
    a2 = dram_in(nc, "a2", [96, 1024])
    g2 = dram_in(nc, "g2", [256, 1024])
    vecs = dram_in(nc, "vecs", [128, 7, NPAIR])
    maskG = dram_in(nc, "maskG", [128, 320])
    I2d = sh["I2"] if sh else dram_in(nc, "I2", [128, 64])
    rmaskd = dram_in(nc, "rmask", [128, TB])
    onesd = dram_in(nc, "onesbd", [128, 128])
    yg = None if sh else nc.dram_tensor("yg", [NPAIR * 128, T], BF16, kind="ExternalOutput").ap()

    if sh:
        pb = sh["pbs"][0:6]; ptb = sh["ptb"]; pq = sh["pbs"][6]
    else:
        pb = [P.ps("pb%d" % i, [128, 512]) for i in range(6)]
        ptb = P.ps("ptb", [128, 1024], BF16)
        pq = P.ps("pq", [128, 512])

    hT = P.sb("hT", [128, KC, T + 1], BF16)
    yT = P.sb("yT", [128, 2, T], BF16)
    ccols = P.sb("ccols", [128, KC], F32)
    ccolb = P.sb("ccolb", [128, KC], BF16)
    badas = P.sb("badas", [128, 32], F32)
    gmixs = P.sb("gmixs", [128, KC], F32)
    mus = P.sb("mus", [128, 6, KC], F32)
    omus = P.sb("omus", [128, 6, KC], F32)
    vecss = P.sb("vecss", [128, 7, NPAIR], F32)
    negs = P.sb("negs", [128, 3, NPAIR], F32)
    maskGs = P.sb("maskGs", [128, 320], F32)
    I2b = P.sb("I2b", [128, 64], BF16)
    I2f = P.sb("I2f", [128, 64], F32)
    rmask = P.sb("rmask", [128, TB], F32)
    onesb = P.sb("onesb", [128, 128], BF16)
    modA = P.sb("modA", [128, 32], F32)
    gsc = P.sb("gsc", [128, KC], F32)
    twh = P.sb("twh", [96, T], BF16)
    ahh = P.sb("ahh", [96, T], BF16)
    ghh = P.sb("ghh", [128, 2, T], BF16)
    w2b = P.sb("w2b", [96, 1024], BF16)
    a2b = P.sb("a2b", [96, 1024], BF16)
    g2b = P.sb("g2b", [128, 2, 1024], BF16)
    onesall = P.sb("onesall", [128, 128], BF16)
    b.memset("pool", onesall[:], 1.0, ["ones_all"])
    Hf = P.sb("Hf", [128, 64], F32)
    Hb = P.sb("Hb", [128, 64], BF16)
    HD = P.sb("HD", [128, 64], F32)

    b.dma("sp", ccols[:], ccol, [], ["ccols"])
    b.dma("pool", ccolb[:], ccol, [], ["ccolb"])
    b.dma("sp", badas[:], badac, [], ["modA_b"])
    b.dma("sp", gmixs[:], gmixc, [], ["gmixs"])
    b.dma("sp", mus[:], muc, [], ["mus"])
    b.dma("sp", vecss[:], vecs, [], ["vecss"])
    b.dma("sp", maskGs[:], maskG, [], ["maskGs"])
    b.dma("pool", I2b[:], I2d, [], ["I2b"])
    b.dma("sp", I2f[:], I2d, [], ["I2f"])
    b.dma("sp", rmask[:], rmaskd, [], ["rmask"])
    b.dma("pool", onesb[:], onesd, [], ["onesb"])
    b.dma("pool", w2b[:], w2, [], ["w2b"])
    b.dma("pool", a2b[:], a2, [], ["a2b"])
    b.dma("pool", g2b[:], g2.rearrange("(c p) n -> p c n", p=128), [], ["g2b"])
    b.ts("dve", omus[:], mus[:], -1.0, 1.0, ALU.mult, ALU.add, ["mus"], ["omus"])
    b.ts("dve", negs[:, 0:2, :], vecss[:, 0:2, :], -1.0, None, ALU.mult, None, ["vecss"], ["negs"])
    b.ts("dve", negs[:, 2, :], vecss[:, 3, :], -1.0, 1.0, ALU.mult, ALU.add, ["vecss"], ["negs"])

    mark0 = P.sb_off
    stage = [P.sb("mstage%d" % i, [128, KC, 256], F32) for i in range(2)]
    stageb = [P.sb("mstageb%d" % i, [128, KC, 256], BF16) for i in range(2)]
    mod_compute(P, b, wada, badas, ccolb, 32, modA, pb[0], stage, stageb, "modA")
    b.ts("dve", gsc[:], modA[:, 16:32], 1.0, None, ALU.add, None, ["modA"], ["gsc"])
    b.tt("dve", gsc[:], gsc[:], gmixs[:], ALU.mult, ["gsc", "gmixs"], ["gsc"])

    XB = 256
    xblk = [P.sb("xblk%d" % i, [128, KC, XB], F32) for i in range(2)]
    xsq = P.sb("xsq", [128, KC, XB], BF16)
    rstd = P.sb("rstd", [128, XB], F32)
    b.memset("dve", hT[:, :, 0:1], 0.0, ["hT"])
    xTv = xT.rearrange("(c p) t -> p c t", p=128)
    for i in range(T // XB):
        xb = xblk[i % 2]
        xk = "xblk%d" % (i % 2)
        t0 = i * XB
        b.dma("sp", xb[:], xTv[:, :, t0:t0 + XB], [], [xk] + ["%s_%d" % (xk, kc) for kc in range(KC)])
        b.act(xsq[:], xb[:], AF.Square, [xk], ["xsq"])
        for kc in range(KC):
            b.mm(pb[1][:, 0:XB], onesall[:], xsq[:, kc, :], kc == 0, kc == KC - 1,
                 ["ones_all", "xsq"], ["pb1"])
        b.act(rstd[:], pb[1][:, 0:XB], AF.Ln, ["pb1"], ["rstd"], bias=1e-6, scale=1.0 / D)
        b.act(rstd[:], rstd[:], AF.Exp, ["rstd"], ["rstd"], scale=-0.5)
        for kc in range(KC):
            xkk = "%s_%d" % (xk, kc)
            b.stt(xb[:, kc, :], xb[:, kc, :], gsc[:, kc:kc + 1], rstd[:], ALU.mult, ALU.mult,
                  [xk, "gsc", "rstd"], [xkk])
            b.act(hT[:, kc, 1 + t0:1 + t0 + XB], xb[:, kc, :], AF.Identity, [xkk, "modA"], ["hT"],
                  bias=modA[:, kc:kc + 1])
    P.barrier()
    P.sb_off = mark0
    import os
    if os.environ.get("STOPA") == "1":
        dbg = nc.dram_tensor("dbg", [128, KC, T + 1], BF16, kind="ExternalOutput").ap()
        b.dma("sp", dbg, hT[:], ["hT"], [])
        return P
    phaseA_rest(P, b, locals())
    return P


def phaseA_rest(P, b, L):
    nc = P.nc
    hT = L["hT"]; yT = L["yT"]; mus = L["mus"]; omus = L["omus"]; vecss = L["vecss"]; negs = L["negs"]
    maskGs = L["maskGs"]; I2b = L["I2b"]; I2f = L["I2f"]; rmask = L["rmask"]; onesb = L["onesb"]
    twh = L["twh"]; ahh = L["ahh"]; ghh = L["ghh"]; w2b = L["w2b"]; a2b = L["a2b"]; g2b = L["g2b"]
    Hf = L["Hf"]; Hb = L["Hb"]; HD = L["HD"]; pb = L["pb"]; ptb = L["ptb"]; pq = L["pq"]
    w_r = L["w_r"]; w_k = L["w_k"]; w_v = L["w_v"]; w1 = L["w1"]; a1 = L["a1"]; g1 = L["g1"]; yg = L["yg"]
    mark0 = L["mark0"]
    MU = {"r": 0, "w": 1, "k": 2, "v": 3, "a": 4, "g": 5}

    def fold(src_dram_cols, ncols, j, dstA, dstB, stg, stgk, keyA, keyB):
        b.dma("sp", stg[:, :, 0:ncols], src_dram_cols.rearrange("(c p) n -> p c n", p=128), [], [stgk])
        mub = bass.AP(tensor=mus, offset=j * KC, ap=[[6 * KC, 128], [1, KC], [0, ncols]])
        omub = bass.AP(tensor=omus, offset=j * KC, ap=[[6 * KC, 128], [1, KC], [0, ncols]])
        b.tt("pool", dstA, stg[:, :, 0:ncols], omub, ALU.mult, [stgk, "omus"], [keyA])
        b.tt("dve", dstB, stg[:, :, 0:ncols], mub, ALU.mult, [stgk, "mus"], [keyB])

    stg = P.sb("stgL", [128, KC, 256], F32)
    l1a = P.sb("l1a", [128, KC, 448], BF16)
    l1b = P.sb("l1b", [128, KC, 448], BF16)
    fold(w1, 96, MU["w"], l1a[:, :, 0:96], l1b[:, :, 0:96], stg, "stgL", "l1a", "l1b")
    fold(a1, 96, MU["a"], l1a[:, :, 96:192], l1b[:, :, 96:192], stg, "stgL", "l1a", "l1b")
    fold(g1, 256, MU["g"], l1a[:, :, 192:448], l1b[:, :, 192:448], stg, "stgL", "l1a", "l1b")
    tmpg = P.sb("tmpg", [128, TB], F32)
    for tb in range(NTB):
        t0 = tb * TB
        for (c0, m, which) in ((0, 96, "w"), (96, 96, "a"), (192, 128, "g0"), (320, 128, "g1")):
            bank = pb[2 + (tb * 4 + ("w", "a", "g0", "g1").index(which)) % 2]
            bk = "pbL%d" % ((tb * 4 + ("w", "a", "g0", "g1").index(which)) % 2)
            for kc in range(KC):
                b.mm(bank[0:m, :], l1a[:, kc, c0:c0 + m], hT[:, kc, 1 + t0:1 + t0 + TB], kc == 0, False,
                     ["l1a", "hT"], [bk])
                b.mm(bank[0:m, :], l1b[:, kc, c0:c0 + m], hT[:, kc, t0:t0 + TB], False, kc == KC - 1,
                     ["l1b", "hT"], [bk])
            if which == "w":
                b.act(twh[:, t0:t0 + TB], bank[0:96, :], AF.Tanh, [bk], ["twh"])
            elif which == "a":
                b.cp("dve", ahh[:, t0:t0 + TB], bank[0:96, :], [bk], ["ahh"])
            else:
                gi = 0 if which == "g0" else 1
                b.act(tmpg[:], bank[:, :], AF.Exp, [bk], ["tmpg"], scale=-1.0)
                b.ts("dve", tmpg[:], tmpg[:], 1.0, None, ALU.add, None, ["tmpg"], ["tmpg"])
                b.rcp(ghh[:, gi, t0:t0 + TB], tmpg[:], ["tmpg"], ["ghh"])
    P.barrier()
    P.sb_off = mark0

    import os
    if os.environ.get("STOPA") == "2":
        dbg = nc.dram_tensor("dbg", [96, T], BF16, kind="ExternalOutput").ap()
        b.dma("sp", dbg, twh[:], ["twh"], [])
        return
    stg3 = P.sb("stg3", [128, KC, 128], F32)
    wf = [[P.sb("wf%d_%d" % (q, i), [128, KC, 128], BF16) for i in range(6)] for q in range(1)]
    def f32t(n):
        return P.sb(n, [128, TB], F32)
    def bft(n):
        return P.sb(n, [128, TB], BF16)
    r_f = f32t("r_f"); k_f = f32t("k_f"); v_f = f32t("v_f"); g_f = f32t("g_f")
    sgw = f32t("sgw"); a_f = f32t("a_f"); cs = f32t("cs"); csx = f32t("csx")
    Ep = f32t("Ep"); Em = f32t("Em"); Ex = f32t("Ex"); kk = f32t("kk"); rn = f32t("rn")
    m_f = f32t("m_f"); kp = f32t("kp"); t1 = f32t("t1"); yraw = f32t("yraw"); t2 = f32t("t2")
    v_b = bft("v_b"); kk2 = bft("kk2"); rt = bft("rt"); kt = bft("kt"); bt = bft("bt"); at = bft("at")
    rkb = bft("rkb"); ysq = bft("ysq"); ybf = bft("ybf")
    tm = P.sb("tm", [128, 3, TB], BF16)
    Gs = P.sb("Gs", [128, NCH, 320], BF16)
    Lv = [P.sb("Lv%d" % i, [128, NCH, 2, 64], BF16) for i in range(2)]
    TTn = [P.sb("TTn%d" % i, [128, NCH, 2, 64], BF16) for i in range(2)]
    Ws = P.sb("Ws", [128, 64], BF16)
    Us = P.sb("Us", [128, 64], BF16)
    print("phaseA sbuf peak", P.sb_off)

    hs2 = [slice(0, 64), slice(64, 128)]
    bankctr = [0]

    nrot = 3 if L.get("sh") else 4
    bg = L["sh"].get("bg") if L.get("sh") else None

    def nextbank():
        i = bankctr[0] % nrot
        bankctr[0] += 1
        return pb[i], "pb%d" % i

    for p in range(int(os.environ.get("PA_NP", NPAIR))):
        q = 0
        cols = slice(p * 128, (p + 1) * 128)
        wfk = ["wf%d_%d" % (q, i) for i in range(6)]
        fold(w_r[:, cols], 128, MU["r"], wf[q][0][:], wf[q][1][:], stg3, "stg3", wfk[0], wfk[1])
        fold(w_k[:, cols], 128, MU["k"], wf[q][2][:], wf[q][3][:], stg3, "stg3", wfk[2], wfk[3])
        fold(w_v[:, cols], 128, MU["v"], wf[q][4][:], wf[q][5][:], stg3, "stg3", wfk[4], wfk[5])
        vc = lambda i: vecss[:, i, p:p + 1]
        b.memset("dve", Hf[:], 0.0, ["Hf"])
        b.memset("dve", Hb[:], 0.0, ["Hb"])
        for tb in range(int(os.environ.get("PA_NTB", NTB))):
            t0 = tb * TB
            tsl = slice(t0, t0 + TB)

            def proj(ia, ib):
                bank, bk = nextbank()
                for kc in range(KC):
                    b.mm(bank[:, :], wf[q][ia][:, kc, :], hT[:, kc, 1 + t0:1 + t0 + TB], kc == 0, False,
                         [wfk[ia], "hT"], [bk])
                    b.mm(bank[:, :], wf[q][ib][:, kc, :], hT[:, kc, t0:t0 + TB], False, kc == KC - 1,
                         [wfk[ib], "hT"], [bk])
                return bank, bk
            bank, bk = proj(0, 1)
            b.cp("act", r_f[:], bank[:, :], [bk], ["r_f"])
            bank, bk = proj(2, 3)
            b.cp("dve", k_f[:], bank[:, :], [bk], ["k_f"])
            bank, bk = proj(4, 5)
            b.cp("act", v_f[:], bank[:, :], [bk], ["v_f"])
            b.cp("dve", v_b[:], bank[:, :], [bk], ["v_b"])
            bank, bk = nextbank()
            b.mm(bank[:, :], w2b[:, cols], twh[:, tsl], True, True, ["w2b", "twh"], [bk])
            b.act(sgw[:], bank[:, :], AF.Exp, [bk, "negs"], ["sgw"], bias=negs[:, 0, p:p + 1], scale=-1.0)
            b.ts("dve", sgw[:], sgw[:], 1.0, None, ALU.add, None, ["sgw"], ["sgw"])
            b.rcp(sgw[:], sgw[:], ["sgw"], ["sgw"])
            bank, bk = nextbank()
            b.mm(bank[:, :], a2b[:, cols], ahh[:, tsl], True, True, ["a2b", "ahh"], [bk])
            b.act(a_f[:], bank[:, :], AF.Exp, [bk, "negs"], ["a_f"], bias=negs[:, 1, p:p + 1], scale=-1.0)
            b.ts("dve", a_f[:], a_f[:], 1.0, None, ALU.add, None, ["a_f"], ["a_f"])
            b.rcp(a_f[:], a_f[:], ["a_f"], ["a_f"])
            bank, bk = nextbank()
            b.mm(bank[:, :], g2b[:, 0, cols], ghh[:, 0, tsl], True, False, ["g2b", "ghh"], [bk])
            b.mm(bank[:, :], g2b[:, 1, cols], ghh[:, 1, tsl], False, True, ["g2b", "ghh"], [bk])
            b.cp("act", g_f[:], bank[:, :], [bk], ["g_f"])
            P.op("dve", lambda e: e.tensor_tensor_scan(out=cs[:], data0=rmask[:], data1=sgw[:], initial=0.0,
                                                      op0=ALU.mult, op1=ALU.add),
                 reads=["rmask", "sgw"], writes=["cs"])
            b.tt("pool", csx[:], cs[:], sgw[:], ALU.subtract, ["cs", "sgw"], ["csx"])
            b.act(Ep[:], cs[:], AF.Exp, ["cs"], ["Ep"], scale=-C1)
            b.act(Em[:], cs[:], AF.Exp, ["cs"], ["Em"], scale=C1)
            b.act(Ex[:], csx[:], AF.Exp, ["csx"], ["Ex"], scale=-C1)
            b.act(kk[:], k_f[:], AF.Copy, ["k_f", "vecss"], ["kk"], scale=vc(2))
            b.act(kk2[:], kk[:], AF.Square, ["kk"], ["kk2"])
            b.mm(pb[4][:, :], onesb[:], kk2[:], True, True, ["onesb", "kk2"], ["pb4"])
            b.act(rn[:], pb[4][:, :], AF.Ln, ["pb4"], ["rn"], bias=1e-24)
            b.act(rn[:], rn[:], AF.Exp, ["rn"], ["rn"], scale=-0.5)
            b.tt("dve", kk[:], kk[:], rn[:], ALU.mult, ["kk", "rn"], ["kk"])
            b.ts("dve", m_f[:], a_f[:], vc(3), negs[:, 2, p:p + 1], ALU.mult, ALU.add, ["a_f", "vecss", "negs"], ["m_f"])
            b.tt("dve", kp[:], k_f[:], m_f[:], ALU.mult, ["k_f", "m_f"], ["kp"])
            b.tt("dve", kt[:], kp[:], Em[:], ALU.mult, ["kp", "Em"], ["kt"])
            b.tt("pool", t1[:], kk[:], a_f[:], ALU.mult, ["kk", "a_f"], ["t1"])
            b.tt("dve", bt[:], t1[:], Em[:], ALU.mult, ["t1", "Em"], ["bt"])
            b.stt(at[:], kk[:], -1.0, Ex[:], ALU.mult, ALU.mult, ["kk", "Ex"], ["at"])
            b.tt("pool", rt[:], r_f[:], Ep[:], ALU.mult, ["r_f", "Ep"], ["rt"])
            b.stt(rkb[:], r_f[:], vc(4), kp[:], ALU.mult, ALU.mult, ["r_f", "kp", "vecss"], ["rkb"])
            b.mm(pb[5][:, :], onesb[:], rkb[:], True, True, ["onesb", "rkb"], ["pb5"])
            b.tt("dve", t2[:], pb[5][:, :], v_f[:], ALU.mult, ["pb5", "v_f"], ["t2"])
            for wi, (src, sk) in enumerate(((v_b, "v_b"), (bt, "bt"), (kt, "kt"))):
                half = wi % 2
                for c in range(NCH):
                    for h in range(2):
                        b.tr(ptb[hs2[h], half * 512 + c * 64: half * 512 + (c + 1) * 64],
                             src[hs2[h], c * 64:(c + 1) * 64], I2b[hs2[h], :], [sk, "I2b"], ["ptb%d" % half])
                b.cp("act" if wi != 1 else "dve", tm[:, wi, :], ptb[:, half * 512:(half + 1) * 512],
                     ["ptb%d" % half], ["tm%d" % wi])
            for c in range(NCH):
                csl = slice(c * 64, (c + 1) * 64)
                g4, g4k = (pq, "pq") if c % 2 == 0 else (pb[1], "pb1")
                for h in range(2):
                    H_ = hs2[h]
                    for gi, (l, lk, r, rk) in enumerate(((bt, "bt", at, "at"), (bt, "bt", rt, "rt"),
                                                          (kt, "kt", at, "at"), (kt, "kt", rt, "rt"),
                                                          (at, "at", bt, "bt"))):
                        b.mm(g4[H_, gi * 64:(gi + 1) * 64], l[H_, csl], r[H_, csl], True, True, [lk, rk], [g4k])
                b.tt("dve", Gs[:, c, :], g4[:, 0:320], maskGs[:], ALU.mult, [g4k, "maskGs"], ["Gs%d" % c])
            allG = ["Gs%d" % c for c in range(NCH)]
            I2bc = bass.AP(tensor=I2f, offset=0, ap=[[64, 128], [0, NCH], [1, 64]])
            b.cp("act", Lv[0][:, :, 0, :], Gs[:, :, 0:64], allG, ["Lv0_%d" % c for c in range(NCH)])
            b.cp("act", Lv[0][:, :, 1, :], Gs[:, :, 256:320], allG, ["Lv0_%d" % c for c in range(NCH)])
            b.tt("dve", TTn[0][:, :, 0, :], Gs[:, :, 0:64], I2bc, ALU.add, allG + ["I2f"],
                 ["TT0_%d" % c for c in range(NCH)])
            b.tt("dve", TTn[0][:, :, 1, :], Gs[:, :, 256:320], I2bc, ALU.add, allG + ["I2f"],
                 ["TT0_%d" % c for c in range(NCH)])
            sqb = [(pb[4], "pb4"), (pb[5], "pb5")]
            ttb = [(pq, "pq"), (pb[0], "pb0")]
            for lvl in range(1, 6):
                src = Lv[(lvl - 1) % 2]; dst = Lv[lvl % 2]
                sk = "Lv%d_" % ((lvl - 1) % 2); dk = "Lv%d_" % (lvl % 2)
                tsrc = TTn[(lvl - 1) % 2]; tdst = TTn[lvl % 2]
                tsk = "TT%d_" % ((lvl - 1) % 2); tdk = "TT%d_" % (lvl % 2)
                last = lvl == 5
                if bg and lvl in (1, 3, 5):
                    bg()
                for hf in range(2):
                    bank, bk = sqb[hf]
                    for cc in range(4):
                        c = hf * 4 + cc
                        for h in range(2):
                            H_ = hs2[h]
                            b.mm(bank[H_, cc * 128:cc * 128 + 64], src[H_, c, 1, :], src[H_, c, 0, :], True, True,
                                 [sk + str(c)], [bk])
                            if not last:
                                b.mm(bank[H_, cc * 128 + 64:cc * 128 + 128], src[H_, c, 0, :], src[H_, c, 1, :],
                                     True, True, [sk + str(c)], [bk])
                for hf in range(2):
                    bank, bk = sqb[hf]
                    cs4 = slice(hf * 4, hf * 4 + 4)
                    pv = bank[:, 0:512].rearrange("p (c a b) -> p c a b", a=2, b=64)
                    keys = [dk + str(c) for c in range(hf * 4, hf * 4 + 4)]
                    if last:
                        b.cp("act", dst[:, cs4, 0, :], pv[:, :, 0, :], [bk], keys)
                    else:
                        b.cp("act", dst[:, cs4, :, :], pv, [bk], keys)
                for hf in range(2):
                    bank, bk = ttb[hf]
                    for cc in range(4):
                        c = hf * 4 + cc
                        for h in range(2):
                            H_ = hs2[h]
                            b.mm(bank[H_, cc * 128:cc * 128 + 64], tsrc[H_, c, 1, :], dst[H_, c, 0, :], True, True,
                                 [tsk + str(c), dk + str(c)], [bk])
                            if not last:
                                b.mm(bank[H_, cc * 128 + 64:cc * 128 + 128], tsrc[H_, c, 0, :], dst[H_, c, 1, :],
                                     True, True, [tsk + str(c), dk + str(c)], [bk])
                for hf in range(2):
                    bank, bk = ttb[hf]
                    cs4 = slice(hf * 4, hf * 4 + 4)
                    pv = bank[:, 0:512].rearrange("p (c a b) -> p c a b", a=2, b=64)
                    rk = [bk] + [tsk + str(c) for c in range(hf * 4, hf * 4 + 4)]
                    wkeys = [tdk + str(c) for c in range(hf * 4, hf * 4 + 4)]
                    if last:
                        b.tt("dve", tdst[:, cs4, 0, :], pv[:, :, 0, :], tsrc[:, cs4, 0, :], ALU.add, rk, wkeys)
                    else:
                        b.tt("dve", tdst[:, cs4, :, :], pv, tsrc[:, cs4, :, :], ALU.add, rk, wkeys)
            TTf = TTn[5 % 2]; TTfk = "TT%d_" % (5 % 2)
            for c in range(NCH):
                csl = slice(c * 64, (c + 1) * 64)
                gk = "Gs%d" % c
                dC = Ep[:, c * 64 + 63:c * 64 + 64]
                b.act(HD[:], Hf[:], AF.Copy, ["Hf", "Ep"], ["HD"], scale=dC)
                if bg and c % 3 == 0:
                    bg()
                for h in range(2):
                    H_ = hs2[h]
                    b.mm(pq[H_, 320:384], at[H_, csl], Hb[H_, :], True, False, ["at", "Hb"], ["pqW"])
                    b.mm(pq[H_, 320:384], Gs[H_, c, 128:192], tm[H_, 0, csl], False, True, [gk, "tm0"], ["pqW"])
                b.cp("act", Ws[:], pq[:, 320:384], ["pqW"], ["Ws"])
                for h in range(2):
                    H_ = hs2[h]
                    b.mm(pq[H_, 384:448], TTf[H_, c, 0, :], Ws[H_, :], True, True, [TTfk + str(c), "Ws"], ["pqU"])
                b.cp("act", Us[:], pq[:, 384:448], ["pqU"], ["Us"])
                for h in range(2):
                    H_ = hs2[h]
                    b.mm(pq[H_, 448:512], tm[H_, 1, csl], Us[H_, :], True, False, ["tm1", "Us"], ["pqH"])
                    b.mm(pq[H_, 448:512], tm[H_, 2, csl], tm[H_, 0, csl], False, True, ["tm2", "tm0"], ["pqH"])
                for h in range(2):
                    H_ = hs2[h]
                    yb = pb[c % 2 + 4 - 4 + 0] if False else pb[4 + (c % 2)]
                    ybk = "pb%d" % (4 + c % 2)
                    b.mm(yb[H_, 0:64], Hb[H_, :], rt[H_, csl], True, False, ["Hb", "rt"], [ybk])
                    b.mm(yb[H_, 0:64], Us[H_, :], Gs[H_, c, 64:128], False, False, ["Us", gk], [ybk])
                    b.mm(yb[H_, 0:64], tm[H_, 0, csl], Gs[H_, c, 192:256], False, True, ["tm0", gk], [ybk])
                b.cp("act", yraw[:, csl], pb[4 + (c % 2)][:, 0:64], ["pb%d" % (4 + c % 2)], ["yraw"])
                b.stt(Hb[:], pq[:, 448:512], dC, HD[:], ALU.mult, ALU.add, ["pqH", "Ep", "HD"], ["Hb"])
                b.stt(Hf[:], pq[:, 448:512], dC, HD[:], ALU.mult, ALU.add, ["pqH", "Ep", "HD"], ["Hf"])
            b.cp("act", ybf[:], yraw[:], ["yraw"], ["ybf"])
            b.act(ysq[:], yraw[:], AF.Square, ["yraw"], ["ysq"])
            b.mm(pb[4][:, :], onesb[:], ybf[:], True, True, ["onesb", "ybf"], ["pb4"])
            b.mm(pb[5][:, :], onesb[:], ysq[:], True, True, ["onesb", "ysq"], ["pb5"])
            mean = cs; var = csx; rs = sgw
            b.ts("dve", mean[:], pb[4][:, :], 1.0 / 64, None, ALU.mult, None, ["pb4"], ["cs"])
            b.act(var[:], mean[:], AF.Square, ["cs"], ["csx"])
            b.stt(var[:], pb[5][:, :], 1.0 / 64, var[:], ALU.mult, ALU.subtract, ["pb5", "csx"], ["csx"])
            b.act(rs[:], var[:], AF.Ln, ["csx"], ["sgw"], bias=64e-5)
            b.act(rs[:], rs[:], AF.Exp, ["sgw"], ["sgw"], scale=-0.5)
            b.tt("dve", yraw[:], yraw[:], mean[:], ALU.subtract, ["yraw", "cs"], ["yraw"])
            b.tt("dve", yraw[:], yraw[:], rs[:], ALU.mult, ["yraw", "sgw"], ["yraw"])
            b.ts("dve", yraw[:], yraw[:], vc(5), vc(6), ALU.mult, ALU.add, ["yraw", "vecss"], ["yraw"])
            b.tt("pool", yraw[:], yraw[:], t2[:], ALU.add, ["yraw", "t2"], ["yraw"])
            b.tt("dve", yT[:, p % 2, tsl], yraw[:], g_f[:], ALU.mult, ["yraw", "g_f"], ["yT%d" % (p % 2)])
        if L.get("sh"):
            sh = L["sh"]
            b.dma("sp", sh["ygbufs"][p].ap().rearrange("(c h) w -> c (h w)", h=2), yT[:, p % 2, :],
                  ["yT%d" % (p % 2)], ["ygbuf%d" % p])
            P.op("pool", lambda e, p=p: e.collective_compute(
                "AllGather", mybir.AluOpType.bypass, replica_groups=sh["RG"],
                ins=[sh["ygbufs"][p].ap().opt()], outs=[sh["ygalls"][p].ap().opt()]),
                reads=["ygbuf%d" % p], writes=["ygall%d" % p], dma=True, cc=True)
        else:
            b.dma("sp", yg[p * 128:(p + 1) * 128, :], yT[:, p % 2, :], ["yT%d" % (p % 2)], ["ygbuf%d" % p])


D = 2048
KC = 16
NT = 1024
TB = 512
DFF = 8192


class Lin:
    def __init__(self, P, b, pbanks, wbuf, tag):
        self.P, self.b, self.pb, self.wbuf, self.tag = P, b, pbanks, wbuf, tag
        self.wi = 0
        self.bi = 0

    def run(self, w, kcn, nfc, xfn, xkey, ntb, evac, fc_group=2):
        P, b = self.P, self.b
        wv = w.rearrange("(c p) n -> p c n", p=128)
        nt = kcn // 16
        for g in range(0, nfc, fc_group):
            if nt == 2 and self.wi % 2 == 1:
                self.wi += 1
            tiles = []
            for j in range(nt):
                ti = (self.wi + j) % 4
                tiles.append((self.wbuf[ti], "%s_w%d" % (self.tag, ti)))
            self.wi += nt
            ncol = fc_group * 128
            for j, (wb, wk) in enumerate(tiles):
                b.dma("pool", wb[:, :, 0:ncol], wv[:, j * 16:(j + 1) * 16, g * 128:g * 128 + ncol], [], [wk])
            for fj in range(fc_group):
                fc = g + fj
                for tb in range(ntb):
                    bank = self.pb[self.bi % len(self.pb)]
                    bk = "pb%d" % (self.bi % len(self.pb))
                    self.bi += 1
                    for kc in range(kcn):
                        wb, wk = tiles[kc // 16]
                        b.mm(bank[:, :], wb[:, kc % 16, fj * 128:(fj + 1) * 128], xfn(kc, tb), kc == 0, kc == kcn - 1,
                             [wk, xkey], [bk])
                    evac(fc, tb, bank, bk)


def rmsnorm_mod(P, b, xres, xkey, gsc, shift, out, outkey, pbank, pkey, onesall, sq, rstd, tmp, extra=None):
    for tb in range(NT // TB):
        ts_ = slice(tb * TB, (tb + 1) * TB)
        for kc in range(KC):
            b.act(sq[:, kc % 2, :], xres[:, kc, ts_], AF.Square, [xkey], ["sq%d" % (kc % 2)])
            b.mm(pbank[:, :], onesall[:], sq[:, kc % 2, :], kc == 0, kc == KC - 1, ["ones_all", "sq%d" % (kc % 2)], [pkey])
        b.act(rstd[:], pbank[:, :], AF.Ln, [pkey], ["rstd"], bias=1e-6, scale=1.0 / D)
        b.act(rstd[:], rstd[:], AF.Exp, ["rstd"], ["rstd"], scale=-0.5)
        for kc in range(KC):
            tv = tmp[:, kc % 2, :]
            tk = "tmp%d" % (kc % 2)
            b.stt(tv, xres[:, kc, ts_], gsc[:, kc:kc + 1], rstd[:], ALU.mult, ALU.mult,
                  [xkey, "gsc", "rstd"], [tk])
            if shift is None:
                b.cp("act", out[:, kc, ts_], tv, [tk], [outkey])
            else:
                b.act(out[:, kc, ts_], tv, AF.Identity, [tk, "mod"], [outkey], bias=shift[:, kc:kc + 1])
            if extra is not None:
                gsc2, shift2, out2, out2key, tmp2 = extra
                tv2 = tmp2[:, kc % 2, :]
                tk2 = "tmpx%d" % (kc % 2)
                b.stt(tv2, xres[:, kc, ts_], gsc2[:, kc:kc + 1], rstd[:], ALU.mult, ALU.mult,
                      [xkey, "gsc2", "rstd"], [tk2])
                b.act(out2[:, kc, ts_], tv2, AF.Identity, [tk2, "mod"], [out2key], bias=shift2[:, kc:kc + 1])


def mlp(P, b, lin, hT, hkey, w_up, w_down, xres, xkey, gate, hid):
    for half in range(2):
        def ev_up(fc, tb, bank, bk):
            dst = hid[:, fc, tb * TB:(tb + 1) * TB]
            b.act(dst, bank[:, :], AF.Relu, [bk], ["hid"])
            b.tt("dve", dst, dst, dst, ALU.mult, ["hid"], ["hid"])
        lin.run(w_up[:, half * 4096:(half + 1) * 4096], KC, 32,
                lambda kc, tb: hT[:, kc, tb * TB:(tb + 1) * TB], hkey, NT // TB, ev_up)

        def ev_dn(fc, tb, bank, bk):
            xs = xres[:, fc, tb * TB:(tb + 1) * TB]
            b.stt(xs, bank[:, :], gate[:, fc:fc + 1], xs, ALU.mult, ALU.add, [bk, "mod", xkey], [xkey])
        lin.run(w_down[half * 4096:(half + 1) * 4096, :], 32, KC,
                lambda kc, tb: hid[:, kc, tb * TB:(tb + 1) * TB], "hid", NT // TB, ev_dn, fc_group=2)


def build_phaseB(nc, P=None, sh=None):
    if P is None:
        P = Prog(nc)
    b = B(P)
    xT = dram_in(nc, "xTown", [D, NT])
    ygT = None if sh else dram_in(nc, "ygT", [D, NT], BF16)
    ccol = sh["ccol"] if sh else dram_in(nc, "ccol", [128, KC])
    if not sh:
        wadaB = dram_in(nc, "wada_b", [D, 8192])
        badaB = dram_in(nc, "bada_b", [128, 64])
        wadaK = dram_in(nc, "wada_kv", [D, 4096])
        badaK = dram_in(nc, "bada_kv", [128, 32])
    gvec = dram_in(nc, "gvecB", [128, 3, KC])
    w_o = dram_in(nc, "w_o", [D, D])
    w_up = dram_in(nc, "w_up0", [D, DFF])
    w_down = dram_in(nc, "w_down0", [DFF, D])
    if sh:
        x2T = sh["x2buf"]; hkvT = sh["hkvbuf"]
        pb = sh["pbs"] + [sh["pbs"][0]]
        P.sb_off = P.base
        msel = dram_in(nc, "msel", [128, 2])
    else:
        x2T = nc.dram_tensor("x2T", [D, NT], F32, kind="ExternalOutput").ap()
        hkvT = nc.dram_tensor("hkvT", [D, NT], BF16, kind="ExternalOutput").ap()
        pb = [P.ps("pb%d" % i, [128, 512]) for i in range(8)]
    xres = P.sb("xres", [128, KC, NT], F32)
    hT = P.sb("hT", [128, KC, NT], BF16)
    hid = P.sb("hid", [128, 32, NT], BF16)
    wbuf = [P.sb("wbuf%d" % i, [128, 16, 256], BF16) for i in range(4)]
    ccolb = P.sb("ccolb", [128, KC], BF16)
    badas = P.sb("badas", [128, 64], F32)
    badks = P.sb("badks", [128, 32], F32)
    gvs = P.sb("gvs", [128, 3, KC], F32)
    gsc2 = P.sb("gsc2", [128, KC], F32)
    tmp2 = P.sb("tmp2", [128, 2, TB], F32)
    if sh:
        modB = sh["modall"][:, 0:64]
        modK = sh["modall"][:, 64:96]
    else:
        modB = P.sb("modB", [128, 64], F32)
        modK = P.sb("modK", [128, 32], F32)
    gsc = P.sb("gsc", [128, KC], F32)
    onesall = P.sb("onesall", [128, 128], BF16)
    sq = P.sb("sq", [128, 2, TB], BF16)
    rstd = P.sb("rstd", [128, TB], F32)
    tmp = P.sb("tmp", [128, 2, TB], F32)
    print("phaseB sbuf", P.sb_off)
    b.memset("pool", onesall[:], 1.0, ["ones_all"])
    b.dma("pool", ccolb[:], ccol, [], ["ccolb"])
    if not sh:
        b.dma("sp", badas[:], badaB, [], ["modB_b"])
        b.dma("sp", badks[:], badaK, [], ["modK_b"])
    b.dma("sp", gvs[:], gvec, [], ["gvs"])
    b.dma("sp", xres[:], xT.rearrange("(c p) t -> p c t", p=128), [], ["xres"])
    if sh:
        mss = P.sb("mss", [128, 2], F32)
        b.dma("sp", mss[:], msel, [], ["mss"])
        hT2 = hid[:, 0:KC, :]
        hTv = hT[:].rearrange("p (r q) t -> p r q t", q=8)
        hT2v = hT2.rearrange("p (r q) t -> p r q t", q=8)
        for p in range(8):
            yv = sh["ygalls"][p].ap().rearrange("(r c h) w -> c r h w", c=128, h=2)
            b.dma("sp", hTv[:, :, p, :], yv[:, :, 0, :], ["ygall%d" % p], ["hT"])
            b.dma("sp", hT2v[:, :, p, :], yv[:, :, 1, :], ["ygall%d" % p], ["hid"])
        for kc in range(KC):
            b.act(hT2[:, kc, :], hT2[:, kc, :], AF.Copy, ["hid", "mss"], ["hid"], scale=mss[:, 1:2])
            b.stt(hT[:, kc, :], hT[:, kc, :], mss[:, 0:1], hT2[:, kc, :], ALU.mult, ALU.add,
                  ["hT", "mss", "hid"], ["hT"])
    else:
        b.dma("sp", hT[:], ygT.rearrange("(c p) t -> p c t", p=128), [], ["hT"])
    if not sh:
        _mod(P, b, wadaB, badas, ccolb, 64, modB, pb[0], "pb0", wbuf, "modB")
        _mod(P, b, wadaK, badks, ccolb, 32, modK, pb[1], "pb1", wbuf, "modK")
    lin = Lin(P, b, pb[2:7], wbuf, "lin")
    def ev_o(fc, tb, bank, bk):
        xs = xres[:, fc, tb * TB:(tb + 1) * TB]
        b.stt(xs, bank[:, :], modB[:, fc:fc + 1], xs, ALU.mult, ALU.add, [bk, "modB", "xres"], ["xres"])
    lin.run(w_o, KC, KC, lambda kc, tb: hT[:, kc, tb * TB:(tb + 1) * TB], "hT", NT // TB, ev_o)
    b.ts("dve", gsc[:], modB[:, 32:48], 1.0, None, ALU.add, None, ["modB"], ["gsc"])
    b.tt("dve", gsc[:], gsc[:], gvs[:, 0, :], ALU.mult, ["gsc", "gvs"], ["gsc"])
    rmsnorm_mod(P, b, xres, "xres", gsc, modB[:, 16:32], hT, "hT", pb[0], "pb0", onesall, sq, rstd, tmp)
    mlp(P, b, lin, hT, "hT", w_up, w_down, xres, "xres", modB[:, 48:64], hid)
    b.dma("sp", x2T.rearrange("(c p) t -> p c t", p=128), xres[:], ["xres"], ["x2buf"])
    b.ts("dve", gsc[:], modK[:, 16:32], 1.0, None, ALU.add, None, ["modK"], ["gsc"])
    b.tt("dve", gsc[:], gsc[:], gvs[:, 1, :], ALU.mult, ["gsc", "gvs"], ["gsc"])
    extra = None
    if sh:
        modC = sh["modall"][:, 96:192]
        b.ts("dve", gsc2[:], modC[:, 16:32], 1.0, None, ALU.add, None, ["modall"], ["gsc2"])
        b.tt("dve", gsc2[:], gsc2[:], gvs[:, 2, :], ALU.mult, ["gsc2", "gvs"], ["gsc2"])
        extra = (gsc2, modC[:, 0:16], hid[:, 0:KC, :], "hid", tmp2)
    rmsnorm_mod(P, b, xres, "xres", gsc, modK[:, 0:16], hT, "hT", pb[0], "pb0", onesall, sq, rstd, tmp, extra=extra)
    if sh:
        b.dma("sp", sh["hqbuf"].rearrange("(c p) t -> p c t", p=128), hid[:, 0:KC, :], ["hid"], ["hqbuf"])
    b.dma("sp", hkvT.rearrange("(c p) t -> p c t", p=128), hT[:], ["hT"], ["hkvbuf"])
    if sh:
        for j in range(2):
            b.dma("sp", sh["tailbufs"][j].ap().rearrange("(c p) t -> p c t", p=128), hT[:, j * 8:(j + 1) * 8, 512:1024],
                  ["hT"], ["tailbuf%d" % j])
            P.op("pool", lambda e, j=j: e.collective_compute(
                "AllGather", mybir.AluOpType.bypass, replica_groups=sh["RG"],
                ins=[sh["tailbufs"][j].ap().opt()], outs=[sh["tailalls"][j].ap().opt()]),
                reads=["tailbuf%d" % j], writes=["tailall%d" % j], dma=True, cc=True)
    return P


def _mod(P, b, wada, bcol, ccolb, nchunks, out_tile, psb, pkey, wbuf, tag):
    wv = wada.rearrange("(c p) n -> p c n", p=128)
    for g in range(nchunks // 2):
        st = wbuf[g % 2]
        sk = "lin_w%d" % (g % 2)
        b.dma("pool", st[:, 0:16, :], wv[:, :, g * 256:(g + 1) * 256], [], [sk])
        for jj in range(2):
            j = g * 2 + jj
            for kc in range(KC):
                b.mm(psb[:, j:j + 1], st[:, kc, jj * 128:(jj + 1) * 128], ccolb[:, kc:kc + 1],
                     kc == 0, kc == KC - 1, [sk, "ccolb"], [pkey])
    b.tt("dve", out_tile[:, 0:nchunks], psb[:, 0:nchunks], bcol[:, 0:nchunks], ALU.add,
         [pkey, tag + "_b"], [tag])


D = 2048
KC = 16
NT = 1024
NE = 1536
TB = 512
DFF = 8192
NPAIRC = 16
BAND = 576


def build_phaseC(nc, P=None, sh=None):
    import os
    if P is None:
        P = Prog(nc)
    else:
        P.sb_off = P.base
    b = B(P)
    x2T = sh["x2buf"] if sh else dram_in(nc, "x2T", [D, NT])
    hkvE = None if sh else dram_in(nc, "hkvE", [D, NE], BF16)
    ccol = sh["ccol"] if sh else dram_in(nc, "ccol", [128, KC])
    if not sh:
        wada1 = dram_in(nc, "wada1", [D, 12288])
        bada1 = dram_in(nc, "bada1", [128, 96])
    gvec = dram_in(nc, "gvecC", [128, 3, KC])
    w_q = dram_in(nc, "w_q", [D, D])
    w_k = dram_in(nc, "w_ks", [D, D])
    w_v = dram_in(nc, "w_vs", [D, D])
    w_o = dram_in(nc, "w_ao", [D, D])
    w_up = dram_in(nc, "w_up1", [D, DFF])
    w_down = dram_in(nc, "w_down1", [DFF, D])
    biasd = dram_in(nc, "biasp", [NPAIRC, 128, BAND])
    mrowd = dram_in(nc, "mrow", [128, 1088])
    I2d = sh["I2"] if sh else dram_in(nc, "I2", [128, 64])
    outT = nc.dram_tensor("outT", [D, NT], F32, kind="ExternalOutput").ap()

    if sh:
        pb = sh["pbs"]; ptb = sh["ptb"]
    else:
        pb = [P.ps("pb%d" % i, [128, 512]) for i in range(7)]
        ptb = P.ps("ptb", [128, 1024], BF16)

    ccolb = P.sb("ccolb", [128, KC], BF16)
    badas = P.sb("badas", [128, 96], F32)
    gvs = P.sb("gvs", [128, 3, KC], F32)
    modC = sh["modall"][:, 96:192] if sh else P.sb("modC", [128, 96], F32)
    gsc = P.sb("gsc", [128, KC], F32)
    onesall = P.sb("onesall", [128, 128], BF16)
    I2b = P.sb("I2b", [128, 64], BF16)
    mrow = P.sb("mrow", [128, 1088], F32)
    rstd = P.sb("rstd", [128, TB], F32)
    tmp = P.sb("tmp", [128, 2, TB], F32)
    sq = P.sb("sq", [128, 2, TB], BF16)
    rs = P.sb("rs", [128, 4], F32)
    rinv = P.sb("rinv", [128, 4], F32)
    base = P.sb_off
    R1 = base
    R2 = R1 + 65536
    R4 = R2 + 32768
    R5 = R4 + 32768
    assert R5 + 65536 <= 229300, R5 + 65536

    def at(off, name, shape, dt):
        P.sb_off = off
        t = P.sb(name, shape, dt)
        return t, P.sb_off

    hkv, _ = at(R1, "hkv", [128, KC, NE], BF16)
    xres, _ = at(R1, "xres", [128, KC, NT], F32)
    wbuf = []
    o = R2
    for i in range(4):
        t, o = at(o, "wbuf%d" % i, [128, 16, 256], BF16)
        wbuf.append(t)
    wqkv = []
    o = R2
    for i in range(2):
        row = []
        for j in range(3):
            t, o = at(o, "wqkv%d_%d" % (i, j), [128, KC, 128], BF16)
            row.append(t)
        wqkv.append(row)
    hT, _ = at(R4, "hT", [128, KC, NT], BF16)
    hid, _ = at(R5, "hid", [128, 32, NT], BF16)
    oT, o = at(R5, "oT", [128, KC, NT], BF16)
    QT, o = at(o, "QT", [128, NT], BF16)
    KT, o = at(o, "KT", [128, NE], BF16)
    VT, o = at(o, "VT", [128, NE], BF16)
    Vtm, o = at(o, "Vtm", [128, 24, 64], BF16)
    biasp = []
    for i in range(2):
        t, o = at(o, "biasp%d" % i, [128, BAND], F32)
        biasp.append(t)
    tS = []
    for i in range(3):
        t, o = at(o, "tS%d" % i, [128, BAND], F32)
        tS.append(t)
    Pn = []
    for i in range(3):
        t, o = at(o, "Pn%d" % i, [128, BAND], BF16)
        Pn.append(t)
    PT = []
    for i in range(2):
        t, o = at(o, "PT%d" % i, [128, BAND], BF16)
        PT.append(t)
    assert o <= R5 + 65536, o
    xblk = []
    o = R5
    for i in range(2):
        t, o = at(o, "xblk%d" % i, [128, KC, 256], F32)
        xblk.append(t)
    print("phaseC sbuf end", R5 + 65536)

    b.memset("pool", onesall[:], 1.0, ["ones_all"])
    b.dma("pool", ccolb[:], ccol, [], ["ccolb"])
    if not sh:
        b.dma("sp", badas[:], bada1, [], ["modC_b"])
    b.dma("sp", gvs[:], gvec, [], ["gvs"])
    b.dma("pool", I2b[:], I2d, [], ["I2b"])
    b.dma("sp", mrow[:], mrowd, [], ["mrow"])
    if sh:
        for j in range(2):
            b.dma("sp", hkv[:, j * 8:(j + 1) * 8, 0:512],
                  sh["tailalls"][j].ap()[0:1024, :].rearrange("(c p) t -> p c t", p=128), ["tailall%d" % j], ["hkv"])
        b.dma("sp", hkv[:, :, 512:NE], sh["hkvbuf"].rearrange("(c p) t -> p c t", p=128), ["hkvbuf"], ["hkv"])
    else:
        b.dma("sp", hkv[:], hkvE.rearrange("(c p) t -> p c t", p=128), [], ["hkv"])
    if not sh:
        _mod(P, b, wada1, badas, ccolb, 96, modC, pb[0], "pb0", wbuf, "modC")
    P.barrier()

    x2v = x2T.rearrange("(c p) t -> p c t", p=128)
    if sh:
        b.dma("sp", hT[:], sh["hqbuf"].rearrange("(c p) t -> p c t", p=128), ["hqbuf"], ["hT"])
    b.ts("dve", gsc[:], modC[:, 16:32], 1.0, None, ALU.add, None, ["modC"], ["gsc"])
    b.tt("dve", gsc[:], gsc[:], gvs[:, 0, :], ALU.mult, ["gsc", "gvs"], ["gsc"])
    x2v = x2T.rearrange("(c p) t -> p c t", p=128)
    XB = 256
    for i in range(0 if sh else NT // XB):
        xb = xblk[i % 2]
        xk = "xblk%d" % (i % 2)
        t0 = i * XB
        b.dma("sp", xb[:], x2v[:, :, t0:t0 + XB], ["x2buf"], [xk])
        for kc in range(KC):
            b.act(sq[:, kc % 2, 0:XB], xb[:, kc, :], AF.Square, [xk], ["sq%d" % (kc % 2)])
            b.mm(pb[1][:, 0:XB], onesall[:], sq[:, kc % 2, 0:XB], kc == 0, kc == KC - 1,
                 ["ones_all", "sq%d" % (kc % 2)], ["pb1"])
        b.act(rstd[:, 0:XB], pb[1][:, 0:XB], AF.Ln, ["pb1"], ["rstd"], bias=1e-6, scale=1.0 / D)
        b.act(rstd[:, 0:XB], rstd[:, 0:XB], AF.Exp, ["rstd"], ["rstd"], scale=-0.5)
        for kc in range(KC):
            b.stt(xb[:, kc, :], xb[:, kc, :], gsc[:, kc:kc + 1], rstd[:, 0:XB], ALU.mult, ALU.mult,
                  [xk, "gsc", "rstd"], [xk])
            b.act(hT[:, kc, t0:t0 + XB], xb[:, kc, :], AF.Identity, [xk, "modC"], ["hT"],
                  bias=modC[:, kc:kc + 1])
    P.barrier()

    hs2 = [slice(0, 64), slice(64, 128)]
    bctr = [0]

    def nextbank():
        i = bctr[0] % 4
        bctr[0] += 1
        return pb[i], "pb%d" % i

    npair = int(os.environ.get("PC_NP", NPAIRC))
    for p in range(npair):
        q = p % 2
        cols = slice(p * 128, (p + 1) * 128)
        wk_ = ["wqkv%d_%d" % (q, j) for j in range(3)]
        for j, w in enumerate((w_q, w_k, w_v)):
            b.dma("pool", wqkv[q][j][:], w[:, cols].rearrange("(c p) n -> p c n", p=128), [], [wk_[j]])
        b.dma("sp", biasp[q][:], biasd[p], [], ["biasp%d" % q])
        for tb in range(NT // TB):
            bank, bk = nextbank()
            for kc in range(KC):
                b.mm(bank[:, :], wqkv[q][0][:, kc, :], hT[:, kc, tb * TB:(tb + 1) * TB], kc == 0, kc == KC - 1,
                     [wk_[0], "hT"], [bk])
            b.act(QT[:, tb * TB:(tb + 1) * TB], bank[:, :], AF.Copy, [bk], ["QT"], scale=0.125)
        for tb in range(NE // TB):
            bank, bk = nextbank()
            for kc in range(KC):
                b.mm(bank[:, :], wqkv[q][1][:, kc, :], hkv[:, kc, tb * TB:(tb + 1) * TB], kc == 0, kc == KC - 1,
                     [wk_[1], "hkv"], [bk])
            b.cp("dve", KT[:, tb * TB:(tb + 1) * TB], bank[:, :], [bk], ["KT"])
        for tb in range(NE // TB):
            bank, bk = nextbank()
            for kc in range(KC):
                b.mm(bank[:, :], wqkv[q][2][:, kc, :], hkv[:, kc, tb * TB:(tb + 1) * TB], kc == 0, kc == KC - 1,
                     [wk_[2], "hkv"], [bk])
            b.cp("act", VT[:, tb * TB:(tb + 1) * TB], bank[:, :], [bk], ["VT"])
        for r0 in range(0, 24, 12):
            for bi in range(12):
                blk = r0 + bi
                for h in range(2):
                    b.tr(ptb[hs2[h], bi * 64:(bi + 1) * 64], VT[hs2[h], blk * 64:(blk + 1) * 64], I2b[hs2[h], :],
                         ["VT", "I2b"], ["ptb"])
            b.cp("dve", Vtm[:, r0:r0 + 12, :], ptb[:, 0:768].rearrange("p (a c) -> p a c", c=64), ["ptb"], ["Vtm"])
        NB3 = 3

        def stage_S(n):
            u = n % NB3
            tSk, Pnk = "tS%d" % u, "Pn%d" % u
            qs = slice(n * 64, (n + 1) * 64)
            k0 = n * 64
            for h in range(2):
                H_ = hs2[h]
                b.mm(pb[4][H_, 0:512], QT[H_, qs], KT[H_, k0:k0 + 512], True, True, ["QT", "KT"], ["pb4"])
                b.mm(pb[5][H_, 0:64], QT[H_, qs], KT[H_, k0 + 512:k0 + 576], True, True, ["QT", "KT"], ["pb5"])
            b.tt("dve", tS[u][:, 0:512], pb[4][:, 0:512], biasp[q][:, 0:512], ALU.add,
                 ["pb4", "biasp%d" % q], [tSk])
            b.tt("dve", tS[u][:, 512:576], pb[5][:, 0:64], biasp[q][:, 512:576], ALU.add,
                 ["pb5", "biasp%d" % q], [tSk])
            if n < 8:
                b.tt("pool", tS[u][:], tS[u][:], mrow[:, n * 64:n * 64 + BAND], ALU.add, [tSk, "mrow"], [tSk])
            P.op("act", lambda e, u=u: e.activation(out=tS[u][:], in_=tS[u][:], func=AF.Exp,
                                                    accum_out=rs[:, u:u + 1]),
                 reads=[tSk], writes=[tSk, "rs%d" % u])
            b.rcp(rinv[:, u:u + 1], rs[:, u:u + 1], ["rs%d" % u], ["rinv%d" % u])
            b.ts("dve", Pn[u][:], tS[u][:], rinv[:, u:u + 1], None, ALU.mult, None, [tSk, "rinv%d" % u], [Pnk])

        def stage_T(n):
            u = n % NB3
            v = n % 2
            Pnk, PTk = "Pn%d" % u, "PT%d" % v
            for blk in range(9):
                for h in range(2):
                    b.tr(ptb[hs2[h], blk * 64:(blk + 1) * 64], Pn[u][hs2[h], blk * 64:(blk + 1) * 64],
                         I2b[hs2[h], :], [Pnk, "I2b"], ["ptb"])
            b.cp("act", PT[v][:], ptb[:, 0:BAND], ["ptb"], [PTk])

        def stage_V(n):
            v = n % 2
            PTk = "PT%d" % v
            qs = slice(n * 64, (n + 1) * 64)
            for h in range(2):
                H_ = hs2[h]
                for blk in range(9):
                    b.mm(pb[6][H_, 0:64], Vtm[H_, n + blk, :], PT[v][H_, blk * 64:(blk + 1) * 64],
                         blk == 0, blk == 8, ["Vtm", PTk], ["pb6"])
            b.cp("act", oT[:, p, qs], pb[6][:, 0:64], ["pb6"], ["oT"])

        NQ = NT // 64
        stage_S(0)
        stage_S(1)
        stage_T(0)
        for n in range(NQ):
            if n + 2 < NQ:
                stage_S(n + 2)
            if n + 1 < NQ:
                stage_T(n + 1)
            stage_V(n)
    P.barrier()

    if os.environ.get("STOPC") == "1":
        dbg = nc.dram_tensor("dbg", [D, NT], BF16, kind="ExternalOutput").ap()
        b.dma("sp", dbg.rearrange("(c p) t -> p c t", p=128), oT[:], ["oT"], [])
        return P

    b.dma("sp", xres[:], x2v, ["x2buf"], ["xres"])
    lin = Lin(P, b, pb[0:7], wbuf, "lin")

    def ev_o(fc, tb, bank, bk):
        xs = xres[:, fc, tb * TB:(tb + 1) * TB]
        b.stt(xs, bank[:, :], modC[:, 32 + fc:33 + fc], xs, ALU.mult, ALU.add, [bk, "modC", "xres"], ["xres"])
    lin.run(w_o, KC, KC, lambda kc, tb: oT[:, kc, tb * TB:(tb + 1) * TB], "oT", NT // TB, ev_o)
    P.barrier()
    b.ts("dve", gsc[:], modC[:, 64:80], 1.0, None, ALU.add, None, ["modC"], ["gsc"])
    b.tt("dve", gsc[:], gsc[:], gvs[:, 1, :], ALU.mult, ["gsc", "gvs"], ["gsc"])
    rmsnorm_mod(P, b, xres, "xres", gsc, modC[:, 48:64], hT, "hT", pb[0], "pb0", onesall, sq, rstd, tmp)
    mlp(P, b, lin, hT, "hT", w_up, w_down, xres, "xres", modC[:, 80:96], hid)
    b.cp("dve", gsc[:], gvs[:, 2, :], ["gvs"], ["gsc"])
    rmsnorm_mod(P, b, xres, "xres", gsc, None, xres, "xres", pb[0], "pb0", onesall, sq, rstd, tmp)
    b.dma("sp", outT.rearrange("(c p) t -> p c t", p=128), xres[:], ["xres"], [])
    return P


def colv(v, n=None):
    v = np.asarray(v, np.float32)
    return np.ascontiguousarray(v.reshape(-1, 128).T)

def consts():
    r = np.arange(128) % 64
    c = np.arange(64)
    lt = (r[:, None] < c[None, :]).astype(np.float32)
    le = (r[:, None] <= c[None, :]).astype(np.float32)
    gt = (r[:, None] > c[None, :]).astype(np.float32)
    maskG = np.concatenate([lt, le, lt, le, gt], axis=1)
    I2 = (r[:, None] == c[None, :]).astype(np.float32)
    rmask = np.ones((128, 512), np.float32); rmask[:, ::64] = 0.0
    onesbd = np.zeros((128, 128), np.float32); onesbd[:64, :64] = 1; onesbd[64:, 64:] = 1
    return dict(maskG=maskG, I2=I2, rmask=rmask, onesbd=onesbd)

def prepA(inp, b, hh):
    cs = slice(hh * 1024, (hh + 1) * 1024)
    g = lambda k: np.asarray(inp[k], np.float32)
    d = dict(consts())
    d["xT"] = np.ascontiguousarray(g("x")[b].T)
    d["ccol"] = colv(g("c")[b])
    d["wada_a"] = np.ascontiguousarray(g("w_ada")[0][:, 0:4096])
    d["bada_a"] = colv(g("b_ada")[0][0:4096])
    d["gmix"] = colv(g("g_mix")[0])
    mu = g("rwkv_mu")[0]
    d["mu"] = np.ascontiguousarray(np.stack([colv(mu[j]) for j in range(6)], axis=1))
    d["w_r"] = np.ascontiguousarray(g("rwkv_w_r")[0][:, cs])
    d["w_k"] = np.ascontiguousarray(g("rwkv_w_k")[0][:, cs])
    d["w_v"] = np.ascontiguousarray(g("rwkv_w_v")[0][:, cs])
    d["w1"] = g("rwkv_w1")[0]; d["a1"] = g("rwkv_a1")[0]; d["g1"] = g("rwkv_g1")[0]
    d["w2"] = np.ascontiguousarray(g("rwkv_w2")[0][:, cs])
    d["a2"] = np.ascontiguousarray(g("rwkv_a2")[0][:, cs])
    d["g2"] = np.ascontiguousarray(g("rwkv_g2")[0][:, cs])
    vs = [g("rwkv_w0")[0], g("rwkv_a0")[0], g("rwkv_k_k")[0], g("rwkv_k_a")[0],
          g("rwkv_r_k")[0].reshape(-1), g("rwkv_ln_w")[0], g("rwkv_ln_b")[0]]
    d["vecs"] = np.ascontiguousarray(np.stack([colv(v[cs]) for v in vs], axis=1))
    return d

def prepC_common(inp):
    g = lambda k: np.asarray(inp[k], np.float32)
    i = np.arange(64)[:, None]; j = np.arange(576)[None, :]
    idx = np.clip(i + 512 - j, -256, 256) + 256
    bias = g("attn_rel_bias")[0][:, idx]
    d = dict(
        wada1=g("w_ada")[1], bada1=colv(g("b_ada")[1]),
        gvecC=np.ascontiguousarray(np.stack([colv(g("g_mix")[1]), colv(g("g_mlp")[1]), colv(g("g_final"))], axis=1)),
        w_q=g("attn_w_q")[0], w_ks=g("w_k_shared"), w_vs=g("w_v_shared"), w_ao=g("attn_w_o")[0],
        w_up1=g("w_up")[1], w_down1=g("w_down")[1],
        biasp=np.ascontiguousarray(bias.reshape(16, 128, 576)), I2=consts()["I2"])
    return d

def mrow_for(s):
    m = np.zeros((128, 1088), np.float32)
    if s == 0:
        m[:, :512] = -1e30
    return m


RG = [[0, 1], [2, 3], [4, 5], [6, 7]]


def build_fused(nc):
    P = Prog(nc)
    pbs = [P.ps("pb%d" % i, [128, 512]) for i in range(7)]
    ptb = P.ps("ptb", [128, 1024], BF16)
    ygbufs = [nc.dram_tensor("ygbuf%d" % p, [256, 1024], BF16) for p in range(8)]
    ygalls = [nc.dram_tensor("ygall%d" % p, [512, 1024], BF16) for p in range(8)]
    x2buf = nc.dram_tensor("x2buf", [2048, 1024], F32)
    hkvbuf = nc.dram_tensor("hkvbuf", [2048, 1024], BF16)
    hqbuf = nc.dram_tensor("hqbuf", [2048, 1024], BF16)
    tailbufs = [nc.dram_tensor("tailbuf%d" % j, [1024, 512], BF16) for j in range(2)]
    tailalls = [nc.dram_tensor("tailall%d" % j, [2048, 512], BF16) for j in range(2)]
    sh = dict(pbs=pbs, ptb=ptb, ccol=dram_in(nc, "ccol", [128, 16]), I2=dram_in(nc, "I2", [128, 64]),
              ygbufs=ygbufs, ygalls=ygalls, x2buf=x2buf.ap(), hkvbuf=hkvbuf.ap(), hqbuf=hqbuf.ap(),
              tailbufs=tailbufs, tailalls=tailalls, RG=RG)
    b = B(P)
    wsrc = [(dram_in(nc, "wada_b", [2048, 8192]), 64), (dram_in(nc, "wada_kv", [2048, 4096]), 32),
            (dram_in(nc, "wada1", [2048, 12288]), 96)]
    badall = dram_in(nc, "bada_all", [128, 192])
    modall = P.sb("modall", [128, 192], F32)
    badalls = P.sb("badalls", [128, 192], F32)
    ccolbg = P.sb("ccolbg", [128, 16], BF16)
    P.base = P.sb_off
    bgst = [P.sb("bgst%d" % i, [128, 16, 128], BF16) for i in range(2)]
    sh["modall"] = modall
    b.dma("sp", badalls[:], badall, [], ["badalls"])
    b.dma("pool", ccolbg[:], sh["ccol"], [], ["ccolbg"])
    groups = []
    j = 0
    for (w, nch) in wsrc:
        wv = w.rearrange("(c p) n -> p c n", p=128)
        for i in range(nch):
            groups.append((wv[:, :, i * 128:(i + 1) * 128], j))
            j += 1
    pbmod = pbs[3]
    state = [0]

    def bg_dma(gi):
        if gi < len(groups):
            src, col = groups[gi]
            b.dma("pool", bgst[gi % 2][:], src, [], ["bgst%d" % (gi % 2)])

    def bg(n=1):
        for _ in range(n):
            gi = state[0]
            if gi >= len(groups):
                return
            state[0] += 1
            src, col = groups[gi]
            st = bgst[gi % 2]
            sk = "bgst%d" % (gi % 2)
            for kc in range(16):
                b.mm(pbmod[:, col:col + 1], st[:, kc, :], ccolbg[:, kc:kc + 1], kc == 0, kc == 15,
                     [sk, "ccolbg"], ["pb3"])
            bg_dma(gi + 2)
    bg_dma(0)
    bg_dma(1)
    sh["bg"] = bg
    build_phaseA(nc, P=P, sh=sh)
    bg(len(groups))
    b.tt("dve", modall[:], pbmod[:, 0:192], badalls[:], ALU.add, ["pb3", "badalls"], ["modall"])
    P.barrier()
    build_phaseB(nc, P=P, sh=sh)
    P.barrier()
    build_phaseC(nc, P=P, sh=sh)
    P.emit()
    P.close()
    return nc


def kernel(**inp):
    g = lambda k: np.asarray(inp[k], np.float32)
    wada_b = np.ascontiguousarray(g("w_ada")[0][:, 4096:12288])
    bada_all = np.ascontiguousarray(np.concatenate(
        [colv(g("b_ada")[0][4096:12288]), colv(g("b_ada_kv")), colv(g("b_ada")[1])], axis=1))
    gvecB = np.ascontiguousarray(np.stack([colv(g("g_mlp")[0]), colv(g("g_kv")), colv(g("g_mix")[1])], axis=1))
    com = prepC_common(inp)
    maps = []
    for c in range(8):
        b_, r = c // 2, c % 2
        tok = slice(r * 1024, (r + 1) * 1024)
        d = prepA(inp, b_, r)
        d.update(com)
        d.pop("bada1", None)
        msel = np.zeros((128, 2), np.float32)
        msel[:, r] = 1.0
        d.update(dict(
            xTown=np.ascontiguousarray(g("x")[b_, tok].T), wada_b=wada_b,
            wada_kv=g("w_ada_kv"), bada_all=bada_all, gvecB=gvecB,
            w_o=g("rwkv_w_o")[0], w_up0=g("w_up")[0], w_down0=g("w_down")[0],
            msel=msel, mrow=mrow_for(r)))
        maps.append(d)
    nc = bass.Bass("TRN2", target_bir_lowering=False)
    build_fused(nc)
    res = run_bass_kernel_spmd(nc, maps, core_ids=list(range(8))).results
    out = np.zeros((4, 2048, 2048), np.float32)
    for c in range(8):
        b_, r = c // 2, c % 2
        out[b_, r * 1024:(r + 1) * 1024, :] = np.asarray(res[c]["outT"]).T
    return out
```
